# Optimizing a Trainium2 kernel written in Bass

```python
import math
import jax, jax.numpy as jnp
from jax import lax
import numpy as np

D_MODEL = 4096
BATCH = 2
SEQ = 4096
DEPTH = 2
DEC_BATCH = 8
DEC_SEQ = 2048
PAST_LEN = 128

MIX_WIDTH = D_MODEL
ATT_WIDTH = MIX_WIDTH // 2
RWKV_WIDTH = MIX_WIDTH - ATT_WIDTH
HEAD_DIM = 128
N_Q_HEADS = ATT_WIDTH // HEAD_DIM
N_KV_HEADS = N_Q_HEADS // 4
KV_WIDTH = N_KV_HEADS * HEAD_DIM
WINDOW = 128
BLOCK = 128
N_REL_BUCKETS = 32
REL_MAX_DIST = 128
RWKV_HEAD = 64
N_RWKV_HEADS = RWKV_WIDTH // RWKV_HEAD
DECAY_LORA = max(32, int(round(1.8 * RWKV_WIDTH ** 0.5 / 32)) * 32)
AAA_LORA = max(32, int(round(1.8 * RWKV_WIDTH ** 0.5 / 32)) * 32)
GATE_LORA = max(32, int(round(0.6 * RWKV_WIDTH ** 0.8 / 32)) * 32)
LNX_EPS = 64e-5
ATT_COLS = ATT_WIDTH + 2 * KV_WIDTH
SHIFT_WIDTH = 3 * RWKV_WIDTH + 2 * DECAY_LORA + 2 * AAA_LORA + GATE_LORA
IN_WIDTH = ATT_COLS + SHIFT_WIDTH
N_MEM = 256
MEM_HEADS = 4
MEM_HEAD_DIM = 128
MEM_WIDTH = MEM_HEADS * MEM_HEAD_DIM
D_FF = -(-8 * D_MODEL // (3 * 256)) * 256
RMS_EPS = 1e-6
NEG = -1e30

kernel_name = "hymba_window_gqa_rwkv7_bidir_encoder"


def _rms(x, g):
    x32 = x.astype(jnp.float32)
    y = x32 * lax.rsqrt(jnp.mean(x32 * x32, axis=-1, keepdims=True) + RMS_EPS)
    return (y * g.astype(jnp.float32)).astype(x.dtype)


def _rel_bucket(rel):
    half = N_REL_BUCKETS // 2
    exact = half // 2
    n = np.abs(rel)
    large = exact + (np.log(np.maximum(n, 1) / exact) / np.log(REL_MAX_DIST / exact)
                     * (half - exact)).astype(np.int32)
    large = np.minimum(large, half - 1)
    return (rel > 0).astype(np.int32) * half + np.where(n < exact, n, large)


def _window_attention(q, k, v, rel_bias, sink):
    B, T = q.shape[:2]
    nb = T // BLOCK
    G = N_Q_HEADS // N_KV_HEADS
    qb = q.reshape(B, nb, BLOCK, N_KV_HEADS, G, HEAD_DIM)

    def windows(t):
        tp = jnp.pad(t, ((0, 0), (BLOCK, BLOCK), (0, 0), (0, 0)))
        tp = tp.reshape(B, nb + 2, BLOCK, N_KV_HEADS, HEAD_DIM)
        return jnp.concatenate([tp[:, :-2], tp[:, 1:-1], tp[:, 2:]], axis=2)

    kw, vw = windows(k), windows(v)
    s = jnp.einsum('bnqkgd,bnskd->bnkgqs', qb, kw).astype(jnp.float32) * (HEAD_DIM ** -0.5)
    qi = np.arange(BLOCK)[:, None]
    kj = np.arange(3 * BLOCK)[None, :]
    rel = kj - BLOCK - qi
    bias = rel_bias.astype(jnp.float32)[_rel_bucket(rel)]
    bias = bias.transpose(2, 0, 1).reshape(N_KV_HEADS, G, BLOCK, 3 * BLOCK)
    kpos = np.arange(nb)[:, None] * BLOCK + np.arange(3 * BLOCK)[None, :] - BLOCK
    valid = (kpos >= 0) & (kpos < T)
    mask = (np.abs(rel) <= WINDOW)[None] & valid[:, None, :]
    s = jnp.where(mask[None, :, None, None], s + bias, NEG)
    sink_l = sink.astype(jnp.float32).reshape(N_KV_HEADS, G, 1, 1)
    m = jnp.maximum(jnp.max(s, axis=-1, keepdims=True), sink_l)
    p = jnp.exp(s - m)
    p = p / (jnp.sum(p, axis=-1, keepdims=True) + jnp.exp(sink_l - m))
    o = jnp.einsum('bnkgqs,bnskd->bnqkgd', p.astype(v.dtype), vw)
    return o.reshape(B, T, ATT_WIDTH)


def _rwkv_step(S, inp):
    r_t, w_t, k_t, v_t, kk_t, b_t = inp
    sa = jnp.einsum('zbhij,zbhj->zbhi', S, -kk_t)
    S = S * w_t[..., None, :] + sa[..., :, None] * b_t[..., None, :] + v_t[..., :, None] * k_t[..., None, :]
    y = jnp.einsum('zbhij,zbhj->zbhi', S, r_t)
    return S, y


def _rwkv_mixer(p, shift_prev, shift_next, w0, w2, a0, a2, g2, k_k, k_a, r_k, lnx_w, lnx_b):
    B, T, _ = p.shape
    H, N, C = N_RWKV_HEADS, RWKV_HEAD, RWKV_WIDTH
    prev = jnp.pad(p, ((0, 0), (1, 0), (0, 0)))[:, :-1]
    nxt = jnp.pad(p, ((0, 0), (0, 1), (0, 0)))[:, 1:]
    p = (p + shift_prev * (prev - p) + shift_next * (nxt - p)).astype(jnp.float32)
    cuts = [C, 2 * C, 3 * C, 3 * C + 2 * DECAY_LORA, 3 * C + 2 * DECAY_LORA + 2 * AAA_LORA]
    r, k, v, wd, ad, gd = jnp.split(p, cuts, axis=-1)
    w_raw = w0 + jnp.einsum('btzl,zlc->btzc', jnp.tanh(wd.reshape(B, T, 2, DECAY_LORA)), w2)
    decay = jnp.exp(-jnp.exp(-jax.nn.softplus(-w_raw) - 0.5))
    a = jax.nn.sigmoid(a0 + jnp.einsum('btzl,zlc->btzc', ad.reshape(B, T, 2, AAA_LORA), a2))
    g = jax.nn.sigmoid(gd) @ g2
    kk = (k * k_k).reshape(B, T, H, N)
    kk = (kk / jnp.maximum(jnp.sqrt(jnp.sum(kk * kk, axis=-1, keepdims=True)), 1e-12)).reshape(B, T, C)
    kd = k[:, :, None] * (1.0 + (a - 1.0) * k_a)
    b = kk[:, :, None] * a

    def shared(t):
        s = jnp.stack([t, t[:, ::-1]], axis=0)
        return s.reshape(2, B, T, H, N).transpose(2, 0, 1, 3, 4)

    def split(t):
        s = jnp.stack([t[:, :, 0], t[:, ::-1, 1]], axis=0)
        return s.reshape(2, B, T, H, N).transpose(2, 0, 1, 3, 4)

    xs = (shared(r), split(decay), split(kd), shared(v), shared(kk), split(b))
    S0 = jnp.zeros((2, B, H, N, N), jnp.float32)
    _, ys = lax.scan(_rwkv_step, S0, xs)
    ys = ys.transpose(1, 2, 0, 3, 4)
    y = ys[0] + ys[1][:, ::-1]
    mu = jnp.mean(y, axis=-1, keepdims=True)
    var = jnp.mean((y - mu) ** 2, axis=-1, keepdims=True)
    y = ((y - mu) * lax.rsqrt(var + LNX_EPS)).reshape(B, T, C) * lnx_w + lnx_b
    bonus = jnp.einsum('bthn,btzhn,hn->bth', r.reshape(B, T, H, N), kd.reshape(B, T, 2, H, N), r_k)
    bonus = (bonus[..., None] * v.reshape(B, T, H, N)).reshape(B, T, C)
    return (y + bonus) * g


def _memory_attention(h, mem, wq, wk, wv, wo, qn, kn):
    B, T, _ = h.shape
    q = _rms((h @ wq).reshape(B, T, MEM_HEADS, MEM_HEAD_DIM), qn)
    k = _rms((mem @ wk).reshape(B, N_MEM, MEM_HEADS, MEM_HEAD_DIM), kn)
    v = (mem @ wv).reshape(B, N_MEM, MEM_HEADS, MEM_HEAD_DIM)
    s = jnp.einsum('bthd,bmhd->bhtm', q, k).astype(jnp.float32) * (MEM_HEAD_DIM ** -0.5)
    p = jax.nn.softmax(s, axis=-1)
    o = jnp.einsum('bhtm,bmhd->bthd', p.astype(v.dtype), v).reshape(B, T, MEM_WIDTH)
    return o @ wo


def _trunk(x, mem, P):
    B, T, _ = x.shape
    for l in range(DEPTH):
        h = _rms(x, P['norm_mix'][l])
        proj = h @ P['w_in'][l]
        q = _rms(proj[..., :ATT_WIDTH].reshape(B, T, N_Q_HEADS, HEAD_DIM), P['q_norm'][l])
        k = _rms(proj[..., ATT_WIDTH:ATT_WIDTH + KV_WIDTH].reshape(B, T, N_KV_HEADS, HEAD_DIM), P['k_norm'][l])
        v = proj[..., ATT_WIDTH + KV_WIDTH:ATT_COLS].reshape(B, T, N_KV_HEADS, HEAD_DIM)
        att = _window_attention(q, k, v, P['rel_bias'], P['sink'][l])
        rw = _rwkv_mixer(proj[..., ATT_COLS:], P['shift_prev'][l], P['shift_next'][l], P['w0'][l], P['w2'][l],
                         P['a0'][l], P['a2'][l], P['g2'][l], P['k_k'][l], P['k_a'][l], P['r_k'][l],
                         P['lnx_w'][l], P['lnx_b'][l]).astype(x.dtype)
        x = x + jnp.concatenate([att, rw], axis=-1) @ P['w_out'][l]
        h = _rms(x, P['norm_mem'][l])
        m = _rms(mem, P['norm_memkv'][l])
        x = x + _memory_attention(h, m, P['wq_mem'][l], P['wk_mem'][l], P['wv_mem'][l], P['wo_mem'][l],
                                  P['qn_mem'][l], P['kn_mem'][l])
        h = _rms(x, P['norm_ffn'][l])
        x = x + (jax.nn.silu(h @ P['w_gate'][l]) * (h @ P['w_up'][l])) @ P['w_down'][l]
    return x


def setup_inputs(seed: int = 0) -> dict:
    key = jax.random.key(seed)
    ks = iter(jax.random.split(key, 48))
    L = DEPTH

    def nrm(shape, scale):
        return jax.random.normal(next(ks), shape, jnp.float32) * scale

    def gain(shape):
        return 1.0 + 0.05 * jax.random.normal(next(ks), shape, jnp.float32)

    def uni(shape, lo, hi):
        return jax.random.uniform(next(ks), shape, jnp.float32, lo, hi)

    return {
        "x_prompt": nrm((BATCH, SEQ, D_MODEL), 1.0),
        "x_sample": nrm((DEC_BATCH, DEC_SEQ, D_MODEL), 1.0),
        "mem_prompt": nrm((BATCH, N_MEM, D_MODEL), 1.0),
        "mem_sample": nrm((DEC_BATCH, N_MEM, D_MODEL), 1.0),
        "rel_bias": nrm((N_REL_BUCKETS, N_Q_HEADS), 0.5),
        "norm_mix": gain((L, D_MODEL)),
        "w_in": nrm((L, D_MODEL, IN_WIDTH), D_MODEL ** -0.5),
        "q_norm": gain((L, HEAD_DIM)),
        "k_norm": gain((L, HEAD_DIM)),
        "sink": nrm((L, N_Q_HEADS), 0.5),
        "shift_prev": uni((L, SHIFT_WIDTH), 0.0, 0.5),
        "shift_next": uni((L, SHIFT_WIDTH), 0.0, 0.5),
        "w0": -0.5 + nrm((L, 2, RWKV_WIDTH), 0.5),
        "w2": nrm((L, 2, DECAY_LORA, RWKV_WIDTH), 0.3 * DECAY_LORA ** -0.5),
        "a0": nrm((L, 2, RWKV_WIDTH), 0.5),
        "a2": nrm((L, 2, AAA_LORA, RWKV_WIDTH), 0.3 * AAA_LORA ** -0.5),
        "g2": nrm((L, GATE_LORA, RWKV_WIDTH), GATE_LORA ** -0.5),
        "k_k": 0.85 + nrm((L, RWKV_WIDTH), 0.05),
        "k_a": gain((L, RWKV_WIDTH)),
        "r_k": nrm((L, N_RWKV_HEADS, RWKV_HEAD), 0.1),
        "lnx_w": gain((L, RWKV_WIDTH)),
        "lnx_b": nrm((L, RWKV_WIDTH), 0.02),
        "w_out": nrm((L, MIX_WIDTH, D_MODEL), MIX_WIDTH ** -0.5),
        "norm_mem": gain((L, D_MODEL)),
        "norm_memkv": gain((L, D_MODEL)),
        "wq_mem": nrm((L, D_MODEL, MEM_WIDTH), D_MODEL ** -0.5),
        "wk_mem": nrm((L, D_MODEL, MEM_WIDTH), D_MODEL ** -0.5),
        "wv_mem": nrm((L, D_MODEL, MEM_WIDTH), D_MODEL ** -0.5),
        "wo_mem": nrm((L, MEM_WIDTH, D_MODEL), MEM_WIDTH ** -0.5),
        "qn_mem": gain((L, MEM_HEAD_DIM)),
        "kn_mem": gain((L, MEM_HEAD_DIM)),
        "norm_ffn": gain((L, D_MODEL)),
        "w_gate": nrm((L, D_MODEL, D_FF), D_MODEL ** -0.5),
        "w_up": nrm((L, D_MODEL, D_FF), D_MODEL ** -0.5),
        "w_down": nrm((L, D_FF, D_MODEL), D_FF ** -0.5),
    }


def reference(x_prompt, x_sample, mem_prompt, mem_sample, rel_bias, norm_mix, w_in, q_norm, k_norm, sink,
              shift_prev, shift_next, w0, w2, a0, a2, g2, k_k, k_a, r_k, lnx_w, lnx_b, w_out,
              norm_mem, norm_memkv, wq_mem, wk_mem, wv_mem, wo_mem, qn_mem, kn_mem,
              norm_ffn, w_gate, w_up, w_down):
    P = dict(rel_bias=rel_bias, norm_mix=norm_mix, w_in=w_in, q_norm=q_norm, k_norm=k_norm, sink=sink,
             shift_prev=shift_prev, shift_next=shift_next, w0=w0, w2=w2, a0=a0, a2=a2, g2=g2,
             k_k=k_k, k_a=k_a, r_k=r_k, lnx_w=lnx_w, lnx_b=lnx_b, w_out=w_out,
             norm_mem=norm_mem, norm_memkv=norm_memkv, wq_mem=wq_mem, wk_mem=wk_mem, wv_mem=wv_mem,
             wo_mem=wo_mem, qn_mem=qn_mem, kn_mem=kn_mem, norm_ffn=norm_ffn,
             w_gate=w_gate, w_up=w_up, w_down=w_down)
    y_prompt = _trunk(x_prompt, mem_prompt, P)
    y_sample = _trunk(x_sample, mem_sample, P)
    return (y_prompt, y_sample)
```

```python
import math
from contextlib import ExitStack
import numpy as np
import concourse.bass as bass
import concourse.mybir as mybir
from concourse.bass_utils import run_bass_kernel_spmd

F32 = mybir.dt.float32
BF16 = mybir.dt.bfloat16
ALU = mybir.AluOpType
AF = mybir.ActivationFunctionType
AX = mybir.AxisListType

FULL_CFG = dict(D=4096, S=2048, ATTW=2048, NQ=16, NKV=4, RW=2048, NH=32, DL=96, AL=96, GL=256,
                DFF=11008, NMEM=256, MEMW=512, MEMH=4, DEPTH=2, NBUCK=32)
RMS_EPS = 1e-6
LNX_EPS = 64e-5
NEG = -1e30
L = 64
SEM_LIMIT = 30000
import os
CSTAGE = float(os.environ.get('CSTAGE', '99'))
NCONST = 128 + 6 * 128


def make_consts_np():
    cst = np.zeros((128, NCONST), np.float32)
    cst[:, 0:128] = np.eye(128, dtype=np.float32)
    s = np.arange(64)[:, None]
    t = np.arange(64)[None, :]
    o = 128
    cst[0:64, o + 0:o + 64] = -1.0 * (s < t)
    cst[0:64, o + 64:o + 128] = (s <= t)
    cst[0:64, o + 128:o + 192] = (s < t)
    cst[0:64, o + 192:o + 256] = (s <= t)
    cst[0:64, o + 256:o + 320] = -1.0 * (t < s)
    cst[0:64, o + 384:o + 448] = -1.0 * (s > t)
    cst[0:64, o + 448:o + 512] = (s >= t)
    cst[0:64, o + 512:o + 576] = (s > t)
    cst[0:64, o + 576:o + 640] = (s >= t)
    cst[0:64, o + 640:o + 704] = -1.0 * (t > s)
    return cst


def derived(c):
    c = dict(c)
    c["T"] = 2 * c["S"]
    c["KVW"] = c["NKV"] * 128
    c["ATT_COLS"] = c["ATTW"] + 2 * c["KVW"]
    c["SHIFTW"] = 3 * c["RW"] + 2 * c["DL"] + 2 * c["AL"] + c["GL"]
    c["INW"] = c["ATT_COLS"] + c["SHIFTW"]
    c["MIXW"] = c["ATTW"] + c["RW"]
    return c


class CSem:
    def __init__(self, k, name, dma=False):
        self.k, self.name, self.dma = k, name, dma
        self.ep = 0
        self.sem = k.nc.alloc_semaphore(f"{name}_e0")
        self.cnt = 0
        self.uid = (name, 0)
        self.final = {}
        self.old = []

    def bump(self):
        step = 16 if self.dma else 1
        if self.cnt + step > SEM_LIMIT:
            self.final[self.uid] = self.cnt
            self.old.append((self, self.sem, self.uid, self.cnt, getattr(self, "seq", 0)))
            self.ep += 1
            self.sem = self.k.nc.alloc_semaphore(f"{self.name}_e{self.ep}")
            self.cnt = 0
            self.uid = (self.name, self.ep)
        self.cnt += step
        self.seq = getattr(self, "seq", 0) + 1
        return (self, self.sem, self.uid, self.cnt, self.seq)


class Obj:
    def __init__(self, name):
        self.name = name
        self.w = []
        self.r = []
        self.dsem = None
        self.pe_acc = False


class Eng:
    def __init__(self, k, name, h):
        self.k, self.name, self.h = k, name, h
        self.cs = CSem(k, "s_" + name)
        self.waited = {}
        self.nops = 0


class K:
    def __init__(self, nc, milestones=None):
        self.nc = nc
        self.ms = milestones
        self.rec = set()
        self.pe = Eng(self, "pe", nc.tensor)
        self.dve = Eng(self, "dve", nc.vector)
        self.act = Eng(self, "act", nc.scalar)
        self.pool = Eng(self, "pool", nc.gpsimd)
        self.sp = Eng(self, "sp", nc.sync)
        self.engs = [self.pe, self.dve, self.act, self.pool, self.sp]
        self.dsems = []
        self.objs = {}
        self.ninst = 0

    def obj(self, name):
        o = self.objs.get(name)
        if o is None:
            o = Obj(name)
            self.objs[name] = o
        return o

    def _wait(self, e, tok):
        cs, sem, uid, val, seq = tok
        if not cs.dma:
            self.rec.add((cs.name, seq))
        if cs.dma:
            val = cs.cnt if cs.uid == uid else cs.final[uid]
        if e.waited.get(uid, 0) >= val:
            return
        e.h.wait_ge(sem, val)
        self.nwait = getattr(self, "nwait", 0) + 1
        e.waited[uid] = val

    def _deps(self, e, reads, writes, pe_acc=False):
        for o in reads:
            for t in o.w:
                self._wait(e, t)
        for o in writes:
            if pe_acc and o.pe_acc and e is self.pe:
                continue
            for t in o.w:
                self._wait(e, t)
            for t in o.r:
                self._wait(e, t)

    @staticmethod
    def _add(lst, tok):
        for i, t in enumerate(lst):
            if t[0] is tok[0]:
                if tok[4] > t[4]:
                    lst[i] = tok
                return
        lst.append(tok)

    def op(self, e, fn, reads=(), writes=(), pe_acc=False):
        reads = [self.obj(o) if isinstance(o, str) else o for o in reads]
        writes = [self.obj(o) if isinstance(o, str) else o for o in writes]
        self._deps(e, reads, writes, pe_acc)
        ins = fn(e.h)
        e.nops += 1
        if self.ms is None or (e.cs.name, e.nops) in self.ms:
            tok = e.cs.bump()
            tok = tok[:4] + (e.nops,)
            ins.then_inc(tok[1], 1)
        else:
            tok = (e.cs, e.cs.sem, e.cs.uid, e.cs.cnt + 1, e.nops)
        self.ninst += 1
        for o in reads:
            self._add(o.r, tok)
        for o in writes:
            if pe_acc and o.pe_acc and e is self.pe:
                o.w = [tok]
            else:
                o.w = [tok]
                o.r = []
            o.pe_acc = e is self.pe
        return ins

    def dma(self, e, out, in_, reads, writes, **kw):
        reads = [self.obj(o) if isinstance(o, str) else o for o in reads]
        writes = [self.obj(o) if isinstance(o, str) else o for o in writes]
        self._deps(e, reads, writes)
        tgt = writes[0]
        if tgt.dsem is None:
            tgt.dsem = CSem(self, "d_" + tgt.name.replace(".", "_").replace("[", "_").replace("]", "_"), dma=True)
            self.dsems.append(tgt.dsem)
        ins = e.h.dma_start(out=out, in_=in_, **kw)
        tok = tgt.dsem.bump()
        ins.then_inc(tok[1], 16)
        self.ninst += 1
        for o in reads:
            self._add(o.r, tok)
        for o in writes:
            o.w = [tok]
            o.r = []
            o.pe_acc = False
        return ins

    def barrier(self):
        p = self.pool
        for e in self.engs:
            if e.nops > 0:
                self.rec.add((e.cs.name, e.nops))
        for e in self.engs:
            if e is not p and e.cs.cnt > 0:
                self._wait(p, (e.cs, e.cs.sem, e.cs.uid, e.cs.cnt, e.nops))
        for ds in self.dsems:
            for tk in ds.old:
                self._wait(p, tk)
            if ds.cnt > 0:
                self._wait(p, (ds, ds.sem, ds.uid, ds.cnt, ds.seq))
        if p.nops > 0 and p.cs.cnt > 0:
            self._wait(p, (p.cs, p.cs.sem, p.cs.uid, p.cs.cnt, p.nops))
        ins = p.h.memset(self.bar_tile[:], 0.0)
        tok = p.cs.bump()
        tok = tok[:4] + (p.nops + 0.5,)
        ins.then_inc(tok[1], 1)
        p.waited[tok[2]] = tok[3]
        for e in self.engs:
            if e is not p:
                self._wait(e, tok)
        for o in self.objs.values():
            o.w, o.r, o.pe_acc = [], [], False


def rel_bucket(rel, nb=32, maxd=128):
    half = nb // 2
    exact = half // 2
    n = np.abs(rel)
    large = exact + (np.log(np.maximum(n, 1) / exact) / np.log(maxd / exact) * (half - exact)).astype(np.int32)
    large = np.minimum(large, half - 1)
    return (rel > 0).astype(np.int32) * half + np.where(n < exact, n, large)


def bias_index_table():
    k = np.arange(128)[:, None]
    q = np.arange(128)[None, :]
    out = np.zeros((3, 128, 128), np.int64)
    for j in range(3):
        rel = k + (j - 1) * 128 - q
        b = rel_bucket(rel)
        out[j] = np.where(np.abs(rel) <= 128, b, 32)
    return out


class Prog:
    def __init__(self, cfg, debug=False, milestones=None):
        self.c = c = derived(cfg)
        self.debug = debug
        self.nc = nc = bass.Bass("TRN2", target_bir_lowering=False)
        self.k = K(nc, milestones)
        self.dbg_outs = {}
        self.build()

    def din(self, name, shape, dt=F32):
        return self.nc.dram_tensor(name, list(shape), dt, kind="ExternalInput").ap()

    def dout(self, name, shape, dt=F32):
        return self.nc.dram_tensor(name, list(shape), dt, kind="ExternalOutput").ap()

    def dscr(self, name, shape, dt=F32):
        if self.debug and name in self.debug_names:
            ap = self.nc.dram_tensor(name, list(shape), dt, kind="ExternalOutput").ap()
            self.dbg_outs[name] = ap
            return ap
        return self.nc.dram_tensor(name, list(shape), dt).ap()

    def sb(self, es, name, shape, dt):
        self._uid = getattr(self, "_uid", 0) + 1
        return es.enter_context(self.nc.sbuf_tensor(f"{name}_u{self._uid}", list(shape), dt))

    def ps(self, es, name, shape, dt=F32):
        self._uid = getattr(self, "_uid", 0) + 1
        return es.enter_context(self.nc.psum_tensor(f"{name}_u{self._uid}", list(shape), dt))

    def build(self):
        c, nc, k = self.c, self.nc, self.k
        D, T, S = c["D"], c["T"], c["S"]
        self.debug_names = {"QT", "KT", "VV", "PT", "MIXT", "XA", "XB", "YF"}
        self.x_in = self.din("x", [T, D])
        self.mem_in = self.din("mem", [2 * c["NMEM"], D])
        self.flag_in = self.din("flag", [128, 1])
        self.bias_in = self.din("biasT", [128, c["NQ"] * 3 * 128])
        self.consts_in = self.din("consts", [128, NCONST])
        self.y_out = self.dout("y", [T, D])
        Ld = c["DEPTH"]
        W = {}
        W["w_in"] = self.din("w_in", [Ld * D, c["INW"]])
        W["w_out"] = self.din("w_out", [Ld * c["MIXW"], D])
        W["wq_mem"] = self.din("wq_mem", [Ld * D, c["MEMW"]])
        W["wk_mem"] = self.din("wk_mem", [Ld * D, c["MEMW"]])
        W["wv_mem"] = self.din("wv_mem", [Ld * D, c["MEMW"]])
        W["wo_mem"] = self.din("wo_mem", [Ld * c["MEMW"], D])
        W["w_gate"] = self.din("w_gate", [Ld * D, c["DFF"]])
        W["w_up"] = self.din("w_up", [Ld * D, c["DFF"]])
        W["w_down"] = self.din("w_down", [Ld * c["DFF"], D])
        W["w2"] = self.din("w2", [Ld * 2 * c["DL"], c["RW"]])
        W["a2"] = self.din("a2", [Ld * 2 * c["AL"], c["RW"]])
        W["g2"] = self.din("g2", [Ld * c["GL"], c["RW"]])
        self.Wf = W
        P = {}
        for nm, shp in [("norm_mix", [Ld, D]), ("norm_mem", [Ld, D]), ("norm_memkv", [Ld, D]), ("norm_ffn", [Ld, D]),
                        ("q_norm", [Ld, 128]), ("k_norm", [Ld, 128]), ("qn_mem", [Ld, 128]), ("kn_mem", [Ld, 128]),
                        ("sink", [Ld, c["NQ"]]), ("shift_prev", [Ld, c["SHIFTW"]]), ("shift_next", [Ld, c["SHIFTW"]]),
                        ("w0", [Ld * 2, c["RW"]]), ("a0", [Ld * 2, c["RW"]]), ("k_k", [Ld, c["RW"]]), ("k_a", [Ld, c["RW"]]),
                        ("r_k", [Ld, c["RW"]]), ("lnx_w", [Ld, c["RW"]]), ("lnx_b", [Ld, c["RW"]])]:
            P[nm] = self.din(nm, shp)
        self.P = P
        self.Wb = {nm: self.dscr(nm + "_b", ap.shape, BF16) for nm, ap in W.items() if nm not in ("w_in", "w_gate", "w_up")}
        KCd = D // 128
        self.Wt = {"w_in": self.dscr("w_in_t", [Ld * 128, KCd * c["INW"]], BF16),
                   "w_gate": self.dscr("w_gate_t", [Ld * 128, KCd * c["DFF"]], BF16),
                   "w_up": self.dscr("w_up_t", [Ld * 128, KCd * c["DFF"]], BF16)}
        self.XA = self.dscr("XA", [T, D])
        self.XB = self.dscr("XB", [T, D])
        self.QT = self.dscr("QT", [c["ATTW"], T], BF16)
        self.KT = self.dscr("KT", [c["KVW"], T], BF16)
        self.VV = self.dscr("VV", [T, c["KVW"]], BF16)
        self.PT = self.dscr("PT", [c["SHIFTW"], T])
        self.YF = self.dscr("YF", [c["RW"], T])
        self.MIXT = self.dscr("MIXT", [c["MIXW"], T], BF16)

        with ExitStack() as es0:
            k.bar_tile = self.sb(es0, "bar_tile", [128, 8], F32)
            self.ident_b = self.sb(es0, "ident_b", [128, 128], BF16)
            self.ident_f = self.sb(es0, "ident_f", [128, 128], F32)
            self.ones_b = self.sb(es0, "ones_b", [128, 128], BF16)
            self.ones_f = self.sb(es0, "ones_f", [128, 128], F32)
            self.flag = self.sb(es0, "flag_sb", [128, 1], F32)
            self.epsc = self.sb(es0, "epsc", [128, 4], F32)
            self.make_consts()
            self.phase_cast()
            cur = self.x_in
            for l in range(c["DEPTH"]):
                last = l == c["DEPTH"] - 1
                self.phase_A(l, cur)
                k.barrier()
                self.phase_B(l)
                k.barrier()
                self.phase_C(l)
                k.barrier()
                x1 = self.XA if cur is not self.XA else self.XB
                self.phase_D(l, cur, x1)
                k.barrier()
                x2 = self.XB if x1 is self.XA else self.XA
                self.phase_E(l, x1, x2)
                k.barrier()
                x3 = self.y_out if last else (self.XA if x2 is self.XB else self.XB)
                self.phase_F(l, x2, x3)
                k.barrier()
                cur = x3

    def make_consts(self):
        k, nc = self.k, self.nc
        k.op(k.pool, lambda h: h.memset(self.ones_f[:], 1.0), writes=["ones_f"])
        k.op(k.pool, lambda h: h.memset(self.ones_b[:], 1.0), writes=["ones_b"])
        k.dma(k.sp, self.ident_f[:], self.consts_in[:, 0:128], reads=[], writes=["ident_f"])
        k.op(k.dve, lambda h: h.tensor_copy(out=self.ident_b[:], in_=self.ident_f[:]), reads=["ident_f"], writes=["ident_b"])
        k.dma(k.sp, self.flag[:], self.flag_in[:, :], reads=[], writes=["flag_sb"])
        k.op(k.pool, lambda h: h.memset(self.epsc[:, 0:1], RMS_EPS), writes=["epsc"])
        k.op(k.pool, lambda h: h.memset(self.epsc[:, 1:2], LNX_EPS), writes=["epsc"])
        k.op(k.pool, lambda h: h.memset(self.epsc[:, 2:3], 1e-24), writes=["epsc"])
        k.op(k.pool, lambda h: h.memset(self.epsc[:, 3:4], 0.0), writes=["epsc"])

    def in_blocks(self):
        c = self.c
        blocks = []
        for h in range(c["NQ"]):
            blocks.append((h * 128, 128, "q", h))
        for h in range(c["NKV"]):
            blocks.append((c["ATTW"] + h * 128, 128, "k", h))
        blocks.append((c["ATTW"] + c["KVW"], c["KVW"], "v", 0))
        col = c["ATT_COLS"]
        widths = [128] * (3 * c["RW"] // 128) + [c["DL"]] * 2 + [c["AL"]] * 2
        g = c["GL"]
        while g > 0:
            widths.append(min(128, g))
            g -= 128
        for wd in widths:
            blocks.append((col, wd, "p", col - c["ATT_COLS"]))
            col += wd
        assert col == c["INW"]
        return blocks

    def wtile(self, nm, l, c0, ncol):
        KC = self.c["D"] // 128
        return self.Wt[nm][l * 128:(l + 1) * 128, KC * c0:KC * (c0 + ncol)].rearrange("p (kc n) -> p kc n", n=ncol)

    def phase_cast(self):
        k = self.k
        c = self.c
        D = c["D"]
        for l in range(c["DEPTH"]):
            for (c0, ncol, kind, aux) in self.in_blocks():
                k.dma(k.pool, self.wtile("w_in", l, c0, ncol),
                      self.Wf["w_in"][l * D:(l + 1) * D, c0:c0 + ncol].rearrange("(kc p) n -> p kc n", p=128), reads=[], writes=["W_w_in"])
            for nm in ("w_gate", "w_up"):
                for f in range(c["DFF"] // 128):
                    k.dma(k.pool, self.wtile(nm, l, f * 128, 128),
                          self.Wf[nm][l * D:(l + 1) * D, f * 128:(f + 1) * 128].rearrange("(kc p) n -> p kc n", p=128), reads=[], writes=["W_" + nm])
        for nm, src in self.Wf.items():
            if nm in self.Wt:
                continue
            dst = self.Wb[nm]
            R = src.shape[0]
            step = 128
            for r0 in range(0, R, step):
                r1 = min(R, r0 + step)
                k.dma(k.pool, dst[r0:r1, :], src[r0:r1, :], reads=[], writes=["W_" + nm])
        k.barrier()

    def norm_transpose(self, es, xsrc, t0, nt, gain_bc, hT, tag, eps=RMS_EPS, xobj=None):
        c, k = self.c, self.k
        D = c["D"]
        KC = D // 128
        st = self._nt_state
        for s in range(nt // 128):
            i = st["i"] % 2
            st["i"] += 1
            xt, hb, ss, pst = st["xt"][i], st["hb"][i], st["ss"][i], st["pst"][i]
            xn, hn, sn, pn = f"nt_xt{i}", f"nt_hb{i}", f"nt_ss{i}", f"nt_ps{i}"
            r0 = t0 + s * 128
            k.dma(k.sp, xt[:], xsrc[r0:r0 + 128, :], reads=[xobj] if xobj else [], writes=[xn])
            k.op(k.act, lambda h: h.activation(out=st["junk"][:], in_=xt[:], func=AF.Square, accum_out=ss[:, 0:1]),
                 reads=[xn], writes=[sn, "nt_junk"])
            k.op(k.act, lambda h: h.activation(out=ss[:, 1:2], in_=ss[:, 0:1], func=AF.Sqrt, scale=1.0 / D, bias=self.epsc[:, 0:1]),
                 reads=[sn, "epsc"], writes=[sn + "b"])
            k.op(k.dve, lambda h: h.reciprocal(out=ss[:, 2:3], in_=ss[:, 1:2]), reads=[sn + "b"], writes=[sn + "c"])
            k.op(k.dve, lambda h: h.scalar_tensor_tensor(out=hb[:], in0=xt[:], scalar=ss[:, 2:3], in1=gain_bc[:],
                                                         op0=ALU.mult, op1=ALU.mult),
                 reads=[xn, sn + "c", tag + "_gain"], writes=[hn])
            for g0 in range(0, KC, 4):
                g1 = min(KC, g0 + 4)
                for kc in range(g0, g1):
                    k.op(k.pe, lambda h: h.transpose(out=pst[:, (kc - g0) * 128:(kc - g0 + 1) * 128],
                                                     in_=hb[:, kc * 128:(kc + 1) * 128], identity=self.ident_b[:]),
                         reads=[hn, "ident_b"], writes=[pn])
                eng = k.act if (g0 // 4) % 2 == 0 else k.dve
                if eng is k.act:
                    k.op(eng, lambda h: h.copy(out=hT[:, g0:g1, s * 128:(s + 1) * 128],
                                               in_=pst[:, 0:(g1 - g0) * 128].rearrange("p (a b) -> p a b", b=128)),
                         reads=[pn], writes=[tag + "_hT"])
                else:
                    k.op(eng, lambda h: h.tensor_copy(out=hT[:, g0:g1, s * 128:(s + 1) * 128],
                                                      in_=pst[:, 0:(g1 - g0) * 128].rearrange("p (a b) -> p a b", b=128)),
                         reads=[pn], writes=[tag + "_hT"])

    def alloc_norm_state(self, es):
        D = self.c["D"]
        self._nt_state = dict(
            i=0,
            xt=[self.sb(es, f"nt_xt{i}", [128, D], F32) for i in range(2)],
            hb=[self.sb(es, f"nt_hb{i}", [128, D], BF16) for i in range(2)],
            ss=[self.sb(es, f"nt_ss{i}", [128, 4], F32) for i in range(2)],
            pst=[self.ps(es, f"nt_ps{i}", [128, 512], BF16) for i in range(2)],
            junk=self.sb(es, "nt_junk", [128, D], BF16),
        )

    def load_bcast_row(self, dst, src_row_ap, n, name):
        k = self.k
        k.dma(k.sp, dst[:], src_row_ap.broadcast(0, 128) if hasattr(src_row_ap, "broadcast") else src_row_ap,
              reads=[], writes=[name])

    def phase_A(self, l, xsrc):
        c, k, nc = self.c, self.k, self.nc
        D, T = c["D"], c["T"]
        KC = D // 128
        NT = min(512, T)
        with ExitStack() as es:
            self.alloc_norm_state(es)
            gain = self.sb(es, "A_gain", [128, D], F32)
            k.dma(k.sp, gain[:], self.P["norm_mix"][l:l + 1, :].partition_broadcast(128), reads=[], writes=["A_gain"])
            hT = self.sb(es, "A_hT", [128, KC, NT], BF16)
            NWB = 3
            wbuf = [self.sb(es, f"A_w{i}", [128, KC, 128], BF16) for i in range(NWB)]
            wv = self.sb(es, "A_wv", [128, KC, c["KVW"]], BF16)
            qg = self.sb(es, "A_qg", [128, 2], F32)
            k.dma(k.sp, qg[:, 0:1], self.P["q_norm"][l:l + 1, :].rearrange("o p -> p o"), reads=[], writes=["A_qg"])
            k.dma(k.sp, qg[:, 1:2], self.P["k_norm"][l:l + 1, :].rearrange("o p -> p o"), reads=[], writes=["A_qg"])
            psm = [self.ps(es, f"A_ps{i}", [128, 512], F32) for i in range(2)]
            ps2 = [self.ps(es, f"A_pq{i}", [128, 512], F32) for i in range(2)]
            sq = [self.sb(es, f"A_sq{i}", [128, NT], BF16) for i in range(2)]
            rs = [self.sb(es, f"A_rs{i}", [128, NT], F32) for i in range(2)]
            ob = [self.sb(es, f"A_ob{i}", [128, NT], BF16) for i in range(2)]
            of = [self.sb(es, f"A_of{i}", [128, NT], F32) for i in range(2)]
            vb = [self.sb(es, f"A_vb{i}", [128, c["KVW"]], BF16) for i in range(2)]
            k.dma(k.sp, wv[:], self.wtile("w_in", l, c["ATTW"] + c["KVW"], c["KVW"]), reads=["W_w_in"], writes=["A_wv"])
            blocks = [b for b in self.in_blocks() if b[2] != "v"]
            cnt = 0
            for t0 in range(0, T, NT):
                self.norm_transpose(es, xsrc, t0, NT, gain, hT, "A", xobj="X_" + str(id(xsrc)))
                for s in range(NT // 128):
                    for n0 in range(0, c["KVW"], 512):
                        n1 = min(c["KVW"], n0 + 512)
                        pv = psm[cnt % 2]
                        pvn = f"A_ps{cnt % 2}"
                        for kc in range(KC):
                            k.op(k.pe, lambda h: h.matmul(pv[:, 0:n1 - n0], lhsT=hT[:, kc, s * 128:(s + 1) * 128],
                                                          rhs=wv[:, kc, n0:n1], start=(kc == 0), stop=(kc == KC - 1)),
                                 reads=["A_hT", "A_wv"], writes=[pvn], pe_acc=(kc > 0))
                        vbi = vb[cnt % 2]
                        k.op(k.act, lambda h: h.copy(out=vbi[:, n0:n1], in_=pv[:, 0:n1 - n0]), reads=[pvn], writes=[f"A_vb{cnt % 2}"])
                        cnt += 1
                    k.dma(k.pool, self.VV[t0 + s * 128:t0 + (s + 1) * 128, :], vb[(cnt - 1) % 2][:],
                          reads=[f"A_vb{(cnt - 1) % 2}"], writes=["VV"])
                for bi, (c0, ncol, kind, aux) in enumerate(blocks):
                    wi = cnt % NWB
                    wt, wn = wbuf[wi], f"A_w{wi}"
                    k.dma(k.sp, wt[:, :, 0:ncol], self.wtile("w_in", l, c0, ncol), reads=["W_w_in"], writes=[wn])
                    pm, pmn = psm[cnt % 2], f"A_ps{cnt % 2}"
                    for kc in range(KC):
                        k.op(k.pe, lambda h: h.matmul(pm[0:ncol, 0:NT], lhsT=wt[:, kc, 0:ncol], rhs=hT[:, kc, :],
                                                      start=(kc == 0), stop=(kc == KC - 1)),
                             reads=["A_hT", wn], writes=[pmn], pe_acc=(kc > 0))
                    j = cnt % 2
                    if kind in ("q", "k"):
                        k.op(k.act, lambda h: h.activation(out=sq[j][:], in_=pm[:, 0:NT], func=AF.Square),
                             reads=[pmn], writes=[f"A_sq{j}"])
                        pq, pqn = ps2[j], f"A_pq{j}"
                        k.op(k.pe, lambda h: h.matmul(pq[:, 0:NT], lhsT=self.ones_b[:], rhs=sq[j][:], start=True, stop=True),
                             reads=[f"A_sq{j}", "ones_b"], writes=[pqn])
                        k.op(k.act, lambda h: h.activation(out=rs[j][:], in_=pq[:, 0:NT], func=AF.Sqrt, scale=1.0 / 128, bias=self.epsc[:, 0:1]),
                             reads=[pqn, "epsc"], writes=[f"A_rs{j}"])
                        k.op(k.dve, lambda h: h.reciprocal(out=rs[j][:], in_=rs[j][:]), reads=[f"A_rs{j}"], writes=[f"A_rs{j}"])
                        gi = 0 if kind == "q" else 1
                        k.op(k.dve, lambda h: h.scalar_tensor_tensor(out=ob[j][:], in0=pm[:, 0:NT], scalar=qg[:, gi:gi + 1], in1=rs[j][:],
                                                                     op0=ALU.mult, op1=ALU.mult),
                             reads=[pmn, f"A_rs{j}", "A_qg"], writes=[f"A_ob{j}"])
                        dst = self.QT if kind == "q" else self.KT
                        k.dma(k.pool, dst[aux * 128:(aux + 1) * 128, t0:t0 + NT], ob[j][:], reads=[f"A_ob{j}"],
                              writes=["QT" if kind == "q" else "KT"])
                    else:
                        k.op(k.act, lambda h: h.copy(out=of[j][0:ncol, :], in_=pm[0:ncol, 0:NT]), reads=[pmn], writes=[f"A_of{j}"])
                        k.dma(k.pool, self.PT[aux:aux + ncol, t0:t0 + NT], of[j][0:ncol, :], reads=[f"A_of{j}"], writes=["PT"])
                    cnt += 1

    def phase_B(self, l):
        c, k = self.c, self.k
        T, S = c["T"], c["S"]
        NB = T // 128
        BPS = S // 128
        G = c["NQ"] // c["NKV"]
        scale = 128 ** -0.5
        with ExitStack() as es:
            BT = self.sb(es, "B_BT", [128, c["NQ"] * 3 * 128], F32)
            k.dma(k.sp, BT[:], self.bias_in[:, :], reads=[], writes=["B_BT"])
            BT4 = BT[:].rearrange("p (h j q) -> p h j q", j=3, q=128)
            esink = self.sb(es, "B_esink", [128, c["NQ"]], F32)
            k.dma(k.sp, esink[:], self.P["sink"][l:l + 1, :].partition_broadcast(128), reads=[], writes=["B_esink"])
            k.op(k.act, lambda h: h.activation(out=esink[:], in_=esink[:], func=AF.Exp), reads=["B_esink"], writes=["B_esink"])
            KTg = self.sb(es, "B_KT", [128, T], BF16)
            Vg = self.sb(es, "B_V", [128, NB, 128], BF16)
            QTh = [self.sb(es, f"B_QT{i}", [128, T], BF16) for i in range(2)]
            OT = [self.sb(es, f"B_OT{i}", [128, T], BF16) for i in range(2)]
            sc = [self.sb(es, f"B_sc{i}", [128, 3, 128], F32) for i in range(2)]
            pT = [self.sb(es, f"B_pT{i}", [128, 3, 128], BF16) for i in range(2)]
            rd = [self.sb(es, f"B_rd{i}", [128, 128], F32) for i in range(2)]
            pS = [self.ps(es, f"B_pS{i}", [128, 512], F32) for i in range(2)]
            pO = [self.ps(es, f"B_pO{i}", [128, 512], F32) for i in range(2)]
            cnt = 0
            hc = 0
            for g in range(c["NKV"]):
                k.dma(k.sp, KTg[:], self.KT[g * 128:(g + 1) * 128, :], reads=["KT"], writes=["B_KT"])
                k.dma(k.sp, Vg[:], self.VV[:, g * 128:(g + 1) * 128].rearrange("(nb p) d -> p nb d", p=128),
                      reads=["VV"], writes=["B_V"])
                for hh in range(G):
                    h_ = g * G + hh
                    qi = hc % 2
                    hc += 1
                    qt, qn = QTh[qi], f"B_QT{qi}"
                    ot, on = OT[qi], f"B_OT{qi}"
                    k.dma(k.sp, qt[:], self.QT[h_ * 128:(h_ + 1) * 128, :], reads=["QT"], writes=[qn])
                    for n in range(NB):
                        i = cnt % 2
                        cnt += 1
                        js = []
                        for j in range(3):
                            kb = n + j - 1
                            if kb < 0 or kb >= NB:
                                continue
                            js.append((j, kb, (kb // BPS) != (n // BPS)))
                        j0, j1 = js[0][0], js[-1][0] + 1
                        ps_, psn = pS[i], f"B_pS{i}"
                        for (j, kb, cross) in js:
                            k.op(k.pe, lambda h: h.matmul(ps_[:, j * 128:(j + 1) * 128], lhsT=KTg[:, kb * 128:(kb + 1) * 128],
                                                          rhs=qt[:, n * 128:(n + 1) * 128], start=True, stop=True),
                                 reads=["B_KT", qn], writes=[psn + f"_{j}"])
                        k.op(k.dve, lambda h: h.scalar_tensor_tensor(
                            out=sc[i][:, j0:j1, :], in0=ps_[:, j0 * 128:j1 * 128].rearrange("p (j q) -> p j q", q=128),
                            scalar=scale, in1=BT4[:, h_, j0:j1, :], op0=ALU.mult, op1=ALU.add),
                            reads=[psn + f"_{j}" for j in range(j0, j1)] + ["B_BT"], writes=[f"B_sc{i}"])
                        k.op(k.act, lambda h: h.activation(out=pT[i][:, j0:j1, :], in_=sc[i][:, j0:j1, :], func=AF.Exp),
                             reads=[f"B_sc{i}"], writes=[f"B_pT{i}"])
                        for (j, kb, cross) in js:
                            if cross:
                                k.op(k.pool, lambda h: h.tensor_scalar(out=pT[i][:, j, :], in0=pT[i][:, j, :], scalar1=self.flag[:, 0:1],
                                                                       scalar2=None, op0=ALU.mult),
                                     reads=[f"B_pT{i}", "flag_sb"], writes=[f"B_pT{i}"])
                        po, pon = pO[i], f"B_pO{i}"
                        for idx, (j, kb, cross) in enumerate(js):
                            k.op(k.pe, lambda h: h.matmul(po[:, 0:128], lhsT=Vg[:, kb, :], rhs=pT[i][:, j, :],
                                                          start=(idx == 0), stop=(idx == len(js) - 1)),
                                 reads=["B_V", f"B_pT{i}"], writes=[pon + "o"], pe_acc=(idx > 0))
                        for idx, (j, kb, cross) in enumerate(js):
                            k.op(k.pe, lambda h: h.matmul(po[:, 128:256], lhsT=self.ones_b[:], rhs=pT[i][:, j, :],
                                                          start=(idx == 0), stop=(idx == len(js) - 1)),
                                 reads=["ones_b", f"B_pT{i}"], writes=[pon + "d"], pe_acc=(idx > 0))
                        k.op(k.dve, lambda h: h.tensor_scalar(out=rd[i][:], in0=po[:, 128:256], scalar1=esink[:, h_:h_ + 1], scalar2=None,
                                                              op0=ALU.add), reads=[pon + "d", "B_esink"], writes=[f"B_rd{i}"])
                        k.op(k.dve, lambda h: h.reciprocal(out=rd[i][:], in_=rd[i][:]), reads=[f"B_rd{i}"], writes=[f"B_rd{i}"])
                        k.op(k.dve, lambda h: h.tensor_tensor(out=ot[:, n * 128:(n + 1) * 128], in0=po[:, 0:128], in1=rd[i][:], op=ALU.mult),
                             reads=[pon + "o", f"B_rd{i}"], writes=[on])
                    k.dma(k.pool, self.MIXT[h_ * 128:(h_ + 1) * 128, :], ot[:], reads=[on], writes=["MIXT"])

    def shift3(self, eng, out_ap, P, W, c0, sp, sn, rd, wr):
        k = self.k
        k.op(eng, lambda h: h.tensor_scalar(out=out_ap, in0=P[:, 1:W + 1], scalar1=c0, scalar2=None, op0=ALU.mult), reads=rd, writes=[wr])
        k.op(eng, lambda h: h.scalar_tensor_tensor(out=out_ap, in0=P[:, 0:W], scalar=sp, in1=out_ap, op0=ALU.mult, op1=ALU.add),
             reads=rd + [wr], writes=[wr])
        k.op(eng, lambda h: h.scalar_tensor_tensor(out=out_ap, in0=P[:, 2:W + 2], scalar=sn, in1=out_ap, op0=ALU.mult, op1=ALU.add),
             reads=rd + [wr], writes=[wr])

    def load_halo(self, P, nr, row0, t0, W, name):
        c, k = self.c, self.k
        T, S = c["T"], c["S"]
        lo = t0 - 1 if t0 > 0 else 0
        hi = t0 + W + 1 if t0 + W < T else T
        if t0 == 0:
            k.op(k.pool, lambda h: h.memset(P[0:nr, 0:1], 0.0), writes=[name])
        if t0 + W >= T:
            k.op(k.pool, lambda h: h.memset(P[0:nr, W + 1:W + 2], 0.0), writes=[name])
        k.dma(k.sp, P[0:nr, lo - (t0 - 1):hi - (t0 - 1)], self.PT[row0:row0 + nr, lo:hi], reads=["PT"], writes=[name])
        if t0 > 0 and t0 % S == 0:
            k.op(k.pool, lambda h: h.tensor_scalar(out=P[0:nr, 0:1], in0=P[0:nr, 0:1], scalar1=self.flag[0:nr, 0:1], scalar2=None, op0=ALU.mult),
                 reads=[name, "flag_sb"], writes=[name])
        if t0 + W < T and (t0 + W) % S == 0:
            k.op(k.pool, lambda h: h.tensor_scalar(out=P[0:nr, W + 1:W + 2], in0=P[0:nr, W + 1:W + 2], scalar1=self.flag[0:nr, 0:1], scalar2=None,
                                                   op0=ALU.mult), reads=[name, "flag_sb"], writes=[name])

    def phase_C(self, l):
        c, k = self.c, self.k
        T, S, RW, NH = c["T"], c["S"], c["RW"], c["NH"]
        DL, AL, GL = c["DL"], c["AL"], c["GL"]
        W = min(512, S)
        NCH = W // L
        NSEG = T // W
        CE = math.exp(-0.5)
        GKC = (GL + 127) // 128
        o_wd = 3 * RW
        o_ad = o_wd + 2 * DL
        o_gd = o_ad + 2 * AL
        with ExitStack() as es:
            TW = [self.sb(es, f"C_TW{z}", [128, T], BF16) for z in range(2)]
            TA = [self.sb(es, f"C_TA{z}", [128, T], BF16) for z in range(2)]
            TG = [self.sb(es, f"C_TG{i}", [128, T], BF16) for i in range(GKC)]
            w2b = self.sb(es, "C_w2b", [128, 2, RW], BF16)
            a2b = self.sb(es, "C_a2b", [128, 2, RW], BF16)
            g2b = self.sb(es, "C_g2b", [128, GKC, RW], BF16)
            for z in range(2):
                k.dma(k.sp, w2b[0:DL, z, :], self.Wb["w2"][(l * 2 + z) * DL:(l * 2 + z + 1) * DL, :], reads=["W_w2"], writes=["C_w2b"])
                k.dma(k.sp, a2b[0:AL, z, :], self.Wb["a2"][(l * 2 + z) * AL:(l * 2 + z + 1) * AL, :], reads=["W_a2"], writes=["C_a2b"])
            for i in range(GKC):
                gk = min(128, GL - i * 128)
                k.dma(k.sp, g2b[0:gk, i, :], self.Wb["g2"][l * GL + i * 128:l * GL + i * 128 + gk, :], reads=["W_g2"], writes=["C_g2b"])
            names = ["sp_r", "sn_r", "sp_k", "sn_k", "sp_v", "sn_v", "w0_0", "w0_1", "a0_0", "a0_1", "k_k", "k_a", "r_k", "lnx_w", "lnx_b",
                     "c0_r", "c0_k", "c0_v", "omka", "omka2"]
            PI = {n: i for i, n in enumerate(names)}
            PRM = self.sb(es, "C_PRM", [64, len(names), NH], F32)
            srcs = {"sp_r": self.P["shift_prev"][l, 0:RW], "sn_r": self.P["shift_next"][l, 0:RW],
                    "sp_k": self.P["shift_prev"][l, RW:2 * RW], "sn_k": self.P["shift_next"][l, RW:2 * RW],
                    "sp_v": self.P["shift_prev"][l, 2 * RW:3 * RW], "sn_v": self.P["shift_next"][l, 2 * RW:3 * RW],
                    "w0_0": self.P["w0"][2 * l, :], "w0_1": self.P["w0"][2 * l + 1, :],
                    "a0_0": self.P["a0"][2 * l, :], "a0_1": self.P["a0"][2 * l + 1, :],
                    "k_k": self.P["k_k"][l, :], "k_a": self.P["k_a"][l, :], "r_k": self.P["r_k"][l, :],
                    "lnx_w": self.P["lnx_w"][l, :], "lnx_b": self.P["lnx_b"][l, :]}
            for n, ap in srcs.items():
                k.dma(k.sp, PRM[:, PI[n], :], ap.rearrange("(h p) -> p h", p=64), reads=[], writes=["C_PRM"], allow_slow_non_contiguous=True)
            for q in "rkv":
                k.op(k.dve, lambda h: h.tensor_tensor(out=PRM[:, PI["c0_" + q], :], in0=PRM[:, PI["sp_" + q], :], in1=PRM[:, PI["sn_" + q], :], op=ALU.add),
                     reads=["C_PRM"], writes=["C_PRM"])
                k.op(k.dve, lambda h: h.tensor_scalar(out=PRM[:, PI["c0_" + q], :], in0=PRM[:, PI["c0_" + q], :], scalar1=-1.0, scalar2=1.0,
                                                      op0=ALU.mult, op1=ALU.add), reads=["C_PRM"], writes=["C_PRM"])
            k.op(k.dve, lambda h: h.tensor_scalar(out=PRM[:, PI["omka"], :], in0=PRM[:, PI["k_a"], :], scalar1=-1.0, scalar2=1.0, op0=ALU.mult, op1=ALU.add),
                 reads=["C_PRM"], writes=["C_PRM"])
            k.op(k.dve, lambda h: h.tensor_scalar(out=PRM[:, PI["omka2"], :], in0=PRM[:, PI["k_a"], :], scalar1=-2.0, scalar2=2.0, op0=ALU.mult, op1=ALU.add),
                 reads=["C_PRM"], writes=["C_PRM"])

            def pc(name, hd):
                return PRM[:, PI[name], hd:hd + 1]
            cst = self.sb(es, "C_cst", [64, 6 * 128], F32)
            k.dma(k.sp, cst[:], self.consts_in[0:64, 128:128 + 6 * 128], reads=[], writes=["C_cst"])
            MA = [self.sb(es, f"C_MA{z}", [64, NCH, 128], BF16) for z in range(2)]
            MB = [self.sb(es, f"C_MB{z}", [64, NCH, 128], BF16) for z in range(2)]
            MC = [self.sb(es, f"C_MC{z}", [64, NCH, 64], BF16) for z in range(2)]
            IDR = self.sb(es, "C_IDR", [64, NCH, 64], BF16)
            for z in range(2):
                o = z * 384
                for ch in range(NCH):
                    k.op(k.dve, lambda h: h.tensor_copy(out=MA[z][:, ch, :], in_=cst[:, o:o + 128]), reads=["C_cst"], writes=["C_M"])
                    k.op(k.dve, lambda h: h.tensor_copy(out=MB[z][:, ch, :], in_=cst[:, o + 128:o + 256]), reads=["C_cst"], writes=["C_M"])
                    k.op(k.dve, lambda h: h.tensor_copy(out=MC[z][:, ch, :], in_=cst[:, o + 256:o + 320]), reads=["C_cst"], writes=["C_M"])
            for ch in range(NCH):
                k.op(k.dve, lambda h: h.tensor_copy(out=IDR[:, ch, :], in_=self.ident_f[0:64, 0:64]), reads=["ident_f"], writes=["C_M"])
            big = [self.ps(es, f"C_big{i}", [128, 1024], F32) for i in range(2)]
            sml = [self.ps(es, f"C_sml{i}", [128, 512], F32) for i in range(2)]
            RY = self.ps(es, "C_RY", [128, 512], F32)
            st = dict(big=0, sml=0)

            def nbig():
                i = st["big"] % 2
                st["big"] += 1
                return big[i], f"C_big{i}"

            def nsml():
                i = st["sml"] % 2
                st["sml"] += 1
                return sml[i], f"C_sml{i}"

            Pl = self.sb(es, "C_Pl", [128, W + 2], F32)
            tl = self.sb(es, "C_tl", [128, W], F32)
            lcol = self.sb(es, "C_lcol", [128, 3], F32)
            blocks = [(o_wd, DL, TW[0], AF.Tanh), (o_wd + DL, DL, TW[1], AF.Tanh), (o_ad, AL, TA[0], AF.Copy), (o_ad + AL, AL, TA[1], AF.Copy)]
            for i in range(GKC):
                gk = min(128, GL - i * 128)
                blocks.append((o_gd + i * 128, gk, TG[i], AF.Sigmoid))
            for (row0, nr, dst, fn) in blocks:
                k.dma(k.sp, lcol[0:nr, 0:1], self.P["shift_prev"][l:l + 1, row0:row0 + nr].rearrange("o p -> p o"), reads=[], writes=["C_lcol"])
                k.dma(k.sp, lcol[0:nr, 1:2], self.P["shift_next"][l:l + 1, row0:row0 + nr].rearrange("o p -> p o"), reads=[], writes=["C_lcol"])
                k.op(k.dve, lambda h: h.tensor_tensor(out=lcol[0:nr, 2:3], in0=lcol[0:nr, 0:1], in1=lcol[0:nr, 1:2], op=ALU.add),
                     reads=["C_lcol"], writes=["C_lcol"])
                k.op(k.dve, lambda h: h.tensor_scalar(out=lcol[0:nr, 2:3], in0=lcol[0:nr, 2:3], scalar1=-1.0, scalar2=1.0, op0=ALU.mult, op1=ALU.add),
                     reads=["C_lcol"], writes=["C_lcol"])
                for sg_ in range(NSEG):
                    t0 = sg_ * W
                    self.load_halo(Pl, nr, row0, t0, W, "C_Pl")
                    self.shift3(k.dve, tl[0:nr, :], Pl[0:nr, :], W, lcol[0:nr, 2:3], lcol[0:nr, 0:1], lcol[0:nr, 1:2], ["C_Pl", "C_lcol"], "C_tl")
                    k.op(k.act, lambda h: h.activation(out=dst[0:nr, t0:t0 + W], in_=tl[0:nr, :], func=fn), reads=["C_tl"], writes=["C_T"])

            if CSTAGE < 1:
                return
            def f32t(n):
                return self.sb(es, "C_" + n, [64, W], F32)

            def bft(n, shp):
                return self.sb(es, "C_" + n, shp, BF16)
            Pr, Pk, Pv = [self.sb(es, "C_P" + q, [64, W + 2], F32) for q in "rkv"]
            r_, kx, v_, sgw, a_, a2_, kk, kd, b_, cA, cB, t1, t2, E1, E2, E3, E4 = [f32t(n) for n in
                ["r", "kx", "v", "sgw", "a", "a2", "kk", "kd", "b", "cA", "cB", "t1", "t2", "E1", "E2", "E3", "E4"]]
            gL = self.sb(es, "C_gL", [64, NCH], F32)
            sqb = bft("sqb", [64, W])
            KR = bft("KR", [64, NCH, 2, 64])
            BK = bft("BK", [64, NCH, 2, 64])
            fmB = bft("fmB", [64, W])
            fmKc = bft("fmKc", [64, W])
            vb = bft("vb", [64, W])
            Z = [bft(f"Z{i}", [64, NCH, 128]) for i in range(2)]
            Zn = bft("Zn", [64, NCH, 128])
            tmB = bft("tmB", [64, NCH, 64])
            tmKc = bft("tmKc", [64, NCH, 64])
            tmV = bft("tmV", [64, NCH, 64])
            SA = bft("SA", [64, NCH, 128])
            SB_ = bft("SB", [64, NCH, 128])
            Nk = [bft(f"Nk{i}", [64, NCH, 64]) for i in range(2)]
            Ak = [bft(f"Ak{i}", [64, NCH, 64]) for i in range(2)]
            NI = [bft(f"NI{i}", [64, NCH, 64]) for i in range(2)]
            RpT = bft("RpT", [64, NCH, 64])
            PcT = self.sb(es, "C_PcT", [64, NCH, 64], F32)
            Qc = self.sb(es, "C_Qc", [64, NCH, 64], F32)
            Hs = [self.sb(es, f"C_H{i}", [64, 64], F32) for i in range(2)]
            Hb = [bft(f"Hb{i}", [64, 64]) for i in range(2)]
            ysb = f32t("ysb")
            yf = f32t("yf")
            ob = bft("ob", [64, W])
            onesf = self.ones_f[0:64, 0:64]
            onesb = self.ones_b[0:64, 0:64]
            idb = self.ident_b[0:64, 0:64]
            idf = self.ident_f[0:64, 0:64]

            def v3(t):
                return t[:, :].rearrange("p (c l) -> p c l", l=L)

            def mm_cols(out_ps, lhsT, rhs_fn, rd, wr, kparts=64):
                for c0 in range(0, W, 512):
                    c1 = min(W, c0 + 512)
                    k.op(k.pe, lambda h: h.matmul(out_ps[0:64, c0:c1], lhsT=lhsT, rhs=rhs_fn(c0, c1), start=True, stop=True), reads=rd, writes=[wr])

            def lora_sig(out_t, wgt, act_tile, nl, z, hd, t0, bias_col, nm):
                ps_, psn = nsml()
                f0 = hd * 64
                mm_cols(ps_, wgt[0:nl, z, f0:f0 + 64], lambda c0, c1: act_tile[0:nl, t0 + c0:t0 + c1], ["C_T", "C_w2b", "C_a2b"], psn)
                k.op(k.act, lambda h: h.activation(out=out_t[:, :], in_=ps_[0:64, 0:W], func=AF.Sigmoid, bias=bias_col), reads=[psn, "C_PRM"], writes=[nm])

            for z in range(2):
                fwd = z == 0
                segs = list(range(NSEG)) if fwd else list(range(NSEG - 1, -1, -1))
                for hd in range(NH):
                    f0 = hd * 64
                    hcnt = 0
                    k.op(k.pool, lambda h: h.memset(Hs[0][:], 0.0), writes=["C_H0"])
                    k.op(k.pool, lambda h: h.memset(Hb[0][:], 0.0), writes=["C_Hb0"])
                    for sg_ in segs:
                        t0 = sg_ * W
                        for q, P_, dst, qi in (("r", Pr, r_, 0), ("k", Pk, kx, 1), ("v", Pv, v_, 2)):
                            self.load_halo(P_, 64, qi * RW + f0, t0, W, "C_P" + q)
                            self.shift3(k.dve, dst[:, :], P_, W, pc("c0_" + q, hd), pc("sp_" + q, hd), pc("sn_" + q, hd),
                                        ["C_P" + q, "C_PRM"], "C_" + q)
                        if CSTAGE < 2:
                            continue
                        lora_sig(sgw, w2b, TW[z], DL, z, hd, t0, pc(f"w0_{z}", hd), "C_sgw")
                        lora_sig(a_, a2b, TA[z], AL, z, hd, t0, pc(f"a0_{z}", hd), "C_a")
                        if CSTAGE < 3:
                            continue
                        k.op(k.dve, lambda h: h.tensor_scalar(out=t1[:, :], in0=kx[:, :], scalar1=pc("k_k", hd), scalar2=None, op0=ALU.mult),
                             reads=["C_k", "C_PRM"], writes=["C_t1"])
                        k.op(k.act, lambda h: h.activation(out=sqb[:, :], in_=t1[:, :], func=AF.Square), reads=["C_t1"], writes=["C_sqb"])
                        ps_, psn = nsml()
                        mm_cols(ps_, onesb, lambda c0, c1: sqb[:, c0:c1], ["C_sqb", "ones_b"], psn)
                        k.op(k.dve, lambda h: h.tensor_scalar(out=t2[:, :], in0=ps_[0:64, 0:W], scalar1=1e-24, scalar2=None, op0=ALU.max),
                             reads=[psn], writes=["C_t2"])
                        k.op(k.act, lambda h: h.activation(out=t2[:, :], in_=t2[:, :], func=AF.Sqrt), reads=["C_t2"], writes=["C_t2"])
                        k.op(k.dve, lambda h: h.reciprocal(out=t2[:, :], in_=t2[:, :]), reads=["C_t2"], writes=["C_t2"])
                        k.op(k.dve, lambda h: h.tensor_tensor(out=kk[:, :], in0=t1[:, :], in1=t2[:, :], op=ALU.mult), reads=["C_t1", "C_t2"], writes=["C_kk"])
                        if CSTAGE < 4:
                            continue
                        k.op(k.pool, lambda h: h.tensor_scalar(out=kd[:, :], in0=a_[:, :], scalar1=pc("k_a", hd), scalar2=pc("omka", hd), op0=ALU.mult, op1=ALU.add),
                             reads=["C_a", "C_PRM"], writes=["C_kd"])
                        k.op(k.pool, lambda h: h.tensor_tensor(out=kd[:, :], in0=kd[:, :], in1=kx[:, :], op=ALU.mult), reads=["C_kd", "C_k"], writes=["C_kd"])
                        k.op(k.pool, lambda h: h.tensor_tensor(out=b_[:, :], in0=kk[:, :], in1=a_[:, :], op=ALU.mult), reads=["C_kk", "C_a"], writes=["C_b"])
                        if CSTAGE < 5:
                            continue
                        src_t, srcn = sgw, "C_sgw"
                        bufs = [(cA, "C_cA"), (cB, "C_cB")]
                        d = 1
                        bi = 0
                        while d < L:
                            dst_t, dstn = bufs[bi]
                            s3, d3 = v3(src_t), v3(dst_t)
                            if fwd:
                                k.op(k.dve, lambda h: h.tensor_tensor(out=d3[:, :, d:L], in0=s3[:, :, d:L], in1=s3[:, :, 0:L - d], op=ALU.add),
                                     reads=[srcn], writes=[dstn])
                                k.op(k.act, lambda h: h.copy(out=d3[:, :, 0:d], in_=s3[:, :, 0:d]), reads=[srcn], writes=[dstn + "x"])
                            else:
                                k.op(k.dve, lambda h: h.tensor_tensor(out=d3[:, :, 0:L - d], in0=s3[:, :, 0:L - d], in1=s3[:, :, d:L], op=ALU.add),
                                     reads=[srcn], writes=[dstn])
                                k.op(k.act, lambda h: h.copy(out=d3[:, :, L - d:L], in_=s3[:, :, L - d:L]), reads=[srcn], writes=[dstn + "x"])
                            src_t, srcn = dst_t, dstn
                            bi ^= 1
                            d *= 2
                        cum, cumn = src_t, srcn
                        cumr = [cumn, cumn + "x"]
                        c3 = v3(cum)
                        le = L - 1 if fwd else 0
                        if CSTAGE < 6:
                            continue
                        k.op(k.act, lambda h: h.activation(out=E1[:, :], in_=cum[:, :], func=AF.Exp, scale=-CE), reads=cumr, writes=["C_E1"])
                        k.op(k.act, lambda h: h.activation(out=E3[:, :], in_=cum[:, :], func=AF.Exp, scale=CE), reads=cumr, writes=["C_E3"])
                        k.op(k.dve, lambda h: h.tensor_tensor(out=t1[:, :], in0=cum[:, :], in1=sgw[:, :], op=ALU.subtract), reads=cumr + ["C_sgw"], writes=["C_t1"])
                        k.op(k.act, lambda h: h.activation(out=E2[:, :], in_=t1[:, :], func=AF.Exp, scale=-CE), reads=["C_t1"], writes=["C_E2"])
                        if CSTAGE < 6.25:
                            continue
                        t23 = v3(t2)
                        for ch in range(NCH):
                            k.op(k.pool, lambda h: h.tensor_scalar(out=t23[:, ch, :], in0=c3[:, ch, :], scalar1=c3[:, ch, le:le + 1], scalar2=None, op0=ALU.subtract),
                                 reads=cumr, writes=["C_t2"])
                        k.op(k.act, lambda h: h.activation(out=E4[:, :], in_=t2[:, :], func=AF.Exp, scale=CE), reads=["C_t2"], writes=["C_E4"])
                        if CSTAGE < 6.5:
                            continue
                        k.op(k.act, lambda h: h.activation(out=gL[:, :], in_=c3[:, :, le], func=AF.Exp, scale=-CE), reads=cumr, writes=["C_gL"])
                        if CSTAGE < 6.75:
                            continue
                        k.op(k.dve, lambda h: h.tensor_tensor(out=KR[:, :, 1, :], in0=v3(r_), in1=v3(E1), op=ALU.mult), reads=["C_r", "C_E1"], writes=["C_KR"])
                        k.op(k.dve, lambda h: h.tensor_tensor(out=KR[:, :, 0, :], in0=v3(kk), in1=v3(E2), op=ALU.mult), reads=["C_kk", "C_E2"], writes=["C_KR"])
                        if CSTAGE < 6.8:
                            continue
                        k.op(k.pool, lambda h: h.tensor_tensor(out=BK[:, :, 0, :], in0=v3(b_), in1=v3(E3), op=ALU.mult), reads=["C_b", "C_E3"], writes=["C_BK"])
                        k.op(k.pool, lambda h: h.tensor_tensor(out=BK[:, :, 1, :], in0=v3(kd), in1=v3(E3), op=ALU.mult), reads=["C_kd", "C_E3"], writes=["C_BK"])
                        if CSTAGE < 6.9:
                            continue
                        k.op(k.dve, lambda h: h.tensor_tensor(out=fmB[:, :], in0=b_[:, :], in1=E4[:, :], op=ALU.mult), reads=["C_b", "C_E4"], writes=["C_fmB"])
                        k.op(k.pool, lambda h: h.tensor_tensor(out=fmKc[:, :], in0=kd[:, :], in1=E4[:, :], op=ALU.mult), reads=["C_kd", "C_E4"], writes=["C_fmKc"])
                        k.op(k.act, lambda h: h.copy(out=vb[:, :], in_=v_[:, :]), reads=["C_v"], writes=["C_vb"])
                        if CSTAGE < 7:
                            continue
                        for qi, (srcf, sn_, dst_ap, dn_) in enumerate([
                                (lambda ch: KR[:, ch, 0, :], "C_KR", Z[0][:, :, 0:64], "C_Z0k"),
                                (lambda ch: fmB[:, ch * 64:(ch + 1) * 64], "C_fmB", tmB[:, :, :], "C_tmB"),
                                (lambda ch: fmKc[:, ch * 64:(ch + 1) * 64], "C_fmKc", tmKc[:, :, :], "C_tmKc"),
                                (lambda ch: vb[:, ch * 64:(ch + 1) * 64], "C_vb", tmV[:, :, :], "C_tmV")]):
                            pt_, ptn = nsml()
                            for ch in range(NCH):
                                k.op(k.pe, lambda h: h.matmul(pt_[0:64, ch * 64:(ch + 1) * 64], lhsT=srcf(ch), rhs=idb, start=True, stop=True),
                                     reads=[sn_, "ident_b"], writes=[ptn])
                            pt3 = pt_[0:64, 0:NCH * 64].rearrange("p (c n) -> p c n", n=64)
                            if qi % 2 == 0:
                                k.op(k.act, lambda h: h.copy(out=dst_ap, in_=pt3), reads=[ptn], writes=[dn_])
                            else:
                                k.op(k.dve, lambda h: h.tensor_copy(out=dst_ap, in_=pt3), reads=[ptn], writes=[dn_])
                        if CSTAGE < 8:
                            continue
                        pa, pan = nbig()
                        for ch in range(NCH):
                            k.op(k.pe, lambda h: h.matmul(pa[0:64, ch * 128:(ch + 1) * 128], lhsT=BK[:, ch, 0, :], rhs=KR[:, ch, :, :].rearrange("p a b -> p (a b)"),
                                                          start=True, stop=True), reads=["C_BK", "C_KR"], writes=[pan])
                        k.op(k.dve, lambda h: h.tensor_tensor(out=SA[:, :, :], in0=pa[0:64, 0:NCH * 128].rearrange("p (c n) -> p c n", n=128), in1=MA[z][:, :, :], op=ALU.mult),
                             reads=[pan, "C_M"], writes=["C_SA"])
                        pb, pbn = nbig()
                        for ch in range(NCH):
                            k.op(k.pe, lambda h: h.matmul(pb[0:64, ch * 128:(ch + 1) * 128], lhsT=BK[:, ch, 1, :], rhs=KR[:, ch, :, :].rearrange("p a b -> p (a b)"),
                                                          start=True, stop=True), reads=["C_BK", "C_KR"], writes=[pbn])
                        k.op(k.dve, lambda h: h.tensor_tensor(out=SB_[:, :, :], in0=pb[0:64, 0:NCH * 128].rearrange("p (c n) -> p c n", n=128), in1=MB[z][:, :, :], op=ALU.mult),
                             reads=[pbn, "C_M"], writes=["C_SB"])
                        pc_, pcn = nsml()
                        for ch in range(NCH):
                            k.op(k.pe, lambda h: h.matmul(pc_[0:64, ch * 64:(ch + 1) * 64], lhsT=KR[:, ch, 0, :], rhs=BK[:, ch, 0, :], start=True, stop=True),
                                 reads=["C_BK", "C_KR"], writes=[pcn])
                        k.op(k.dve, lambda h: h.tensor_tensor(out=Ak[0][:, :, :], in0=pc_[0:64, 0:NCH * 64].rearrange("p (c n) -> p c n", n=64), in1=MC[z][:, :, :], op=ALU.mult),
                             reads=[pcn, "C_M"], writes=["C_Ak0"])
                        k.op(k.act, lambda h: h.copy(out=Nk[0][:, :, :], in_=SA[:, :, 0:64]), reads=["C_SA"], writes=["C_Nk0"])
                        k.op(k.dve, lambda h: h.tensor_tensor(out=NI[0][:, :, :], in0=SA[:, :, 0:64], in1=IDR[:, :, :], op=ALU.add), reads=["C_SA", "C_M"], writes=["C_NI0"])
                        if CSTAGE < 9:
                            continue
                        px, pxn = nsml()
                        for ch in range(NCH):
                            k.op(k.pe, lambda h: h.matmul(px[0:64, ch * 64:(ch + 1) * 64], lhsT=SB_[:, ch, 0:64], rhs=tmV[:, ch, :], start=True, stop=True),
                                 reads=["C_SB", "C_tmV"], writes=[pxn])
                        k.op(k.act, lambda h: h.copy(out=Z[0][:, :, 64:128], in_=px[0:64, 0:NCH * 64].rearrange("p (c n) -> p c n", n=64)), reads=[pxn], writes=["C_Z0x"])
                        if CSTAGE < 10:
                            continue
                        zr = ["C_Z0k", "C_Z0x"]
                        for kk_ in range(6):
                            ci, ni = kk_ % 2, (kk_ + 1) % 2
                            if CSTAGE < 10.1 + kk_:
                                break
                            if kk_ < 5:
                                pn, pnn = nsml()
                                for ch in range(NCH):
                                    k.op(k.pe, lambda h: h.matmul(pn[0:64, ch * 64:(ch + 1) * 64], lhsT=Ak[ci][:, ch, :], rhs=Nk[ci][:, ch, :], start=True, stop=True),
                                         reads=[f"C_Ak{ci}", f"C_Nk{ci}"], writes=[pnn])
                                pn3 = pn[0:64, 0:NCH * 64].rearrange("p (c n) -> p c n", n=64)
                                if CSTAGE < 10.12 + kk_:
                                    break
                                k.op(k.act, lambda h: h.copy(out=Nk[ni][:, :, :], in_=pn3), reads=[pnn], writes=[f"C_Nk{ni}"])
                                if CSTAGE < 10.14 + kk_:
                                    break
                                k.op(k.dve, lambda h: h.tensor_tensor(out=NI[ni][:, :, :], in0=Nk[ni][:, :, :], in1=IDR[:, :, :], op=ALU.add), reads=[f"C_Nk{ni}", "C_M"], writes=[f"C_NI{ni}"])
                                if CSTAGE < 10.16 + kk_:
                                    break
                                if kk_ < 4:
                                    pa2, pa2n = nsml()
                                    for ch in range(NCH):
                                        k.op(k.pe, lambda h: h.matmul(pa2[0:64, ch * 64:(ch + 1) * 64], lhsT=Nk[ci][:, ch, :], rhs=Ak[ci][:, ch, :], start=True, stop=True),
                                             reads=[f"C_Ak{ci}", f"C_Nk{ci}"], writes=[pa2n])
                                    k.op(k.act, lambda h: h.copy(out=Ak[ni][:, :, :], in_=pa2[0:64, 0:NCH * 64].rearrange("p (c n) -> p c n", n=64)),
                                         reads=[pa2n], writes=[f"C_Ak{ni}"])
                            if CSTAGE < 10.5 + kk_:
                                break
                            pz, pzn = nbig()
                            for ch in range(NCH):
                                k.op(k.pe, lambda h: h.matmul(pz[0:64, ch * 128:(ch + 1) * 128], lhsT=NI[ci][:, ch, :], rhs=Z[ci][:, ch, :], start=True, stop=True),
                                     reads=[f"C_NI{ci}"] + zr, writes=[pzn])
                            pz3 = pz[0:64, 0:NCH * 128].rearrange("p (c n) -> p c n", n=128)
                            if kk_ < 5:
                                k.op(k.act, lambda h: h.copy(out=Z[ni][:, :, :], in_=pz3), reads=[pzn], writes=[f"C_Zf{ni}"])
                                zr = [f"C_Zf{ni}"]
                            else:
                                k.op(k.act, lambda h: h.mul(out=Zn[:, :, :], in_=pz3, mul=-1.0), reads=[pzn], writes=["C_Zn"])
                        if CSTAGE < 12:
                            continue
                        pp, ppn = nsml()
                        for ch in range(NCH):
                            k.op(k.pe, lambda h: h.matmul(pp[0:64, ch * 64:(ch + 1) * 64], lhsT=Zn[:, ch, 0:64], rhs=tmB[:, ch, :], start=True, stop=True),
                                 reads=["C_Zn", "C_tmB"], writes=[ppn])
                        for ch in range(NCH):
                            k.op(k.dve, lambda h: h.scalar_tensor_tensor(out=PcT[:, ch, :], in0=idf, scalar=gL[:, ch:ch + 1], in1=pp[0:64, ch * 64:(ch + 1) * 64],
                                                                         op0=ALU.mult, op1=ALU.add), reads=[ppn, "C_gL", "ident_f"], writes=["C_PcT"])
                        if CSTAGE < 13:
                            continue
                        pq_, pqn = nsml()
                        for ch in range(NCH):
                            k.op(k.pe, lambda h: h.matmul(pq_[0:64, ch * 64:(ch + 1) * 64], lhsT=tmKc[:, ch, :], rhs=tmV[:, ch, :], start=True, stop=False),
                                 reads=["C_tmKc", "C_tmV"], writes=[pqn])
                            k.op(k.pe, lambda h: h.matmul(pq_[0:64, ch * 64:(ch + 1) * 64], lhsT=tmB[:, ch, :], rhs=Zn[:, ch, 64:128], start=False, stop=True),
                                 reads=["C_tmB", "C_Zn"], writes=[pqn], pe_acc=True)
                        k.op(k.act, lambda h: h.copy(out=Qc[:, :, :], in_=pq_[0:64, 0:NCH * 64].rearrange("p (c n) -> p c n", n=64)), reads=[pqn], writes=["C_Qc"])
                        if CSTAGE < 14:
                            continue
                        pr_, prn = nsml()
                        for ch in range(NCH):
                            k.op(k.pe, lambda h: h.matmul(pr_[0:64, ch * 64:(ch + 1) * 64], lhsT=Zn[:, ch, 0:64], rhs=SA[:, ch, 64:128], start=True, stop=True),
                                 reads=["C_Zn", "C_SA"], writes=[prn])
                        k.op(k.dve, lambda h: h.tensor_tensor(out=RpT[:, :, :], in0=pr_[0:64, 0:NCH * 64].rearrange("p (c n) -> p c n", n=64), in1=KR[:, :, 1, :], op=ALU.add),
                             reads=[prn, "C_KR"], writes=["C_RpT"])
                        if CSTAGE < 15:
                            continue
                        chs = list(range(NCH)) if fwd else list(range(NCH - 1, -1, -1))
                        for ch in chs:
                            tg = t0 + ch * L
                            hi_, ho_ = hcnt % 2, (hcnt + 1) % 2
                            hcnt += 1
                            boundary = (fwd and tg == S) or ((not fwd) and tg + L == S)
                            if boundary:
                                k.op(k.dve, lambda h: h.tensor_scalar(out=Hs[hi_][:], in0=Hs[hi_][:], scalar1=self.flag[0:64, 0:1], scalar2=None, op0=ALU.mult),
                                     reads=[f"C_H{hi_}", "flag_sb"], writes=[f"C_H{hi_}"])
                                k.op(k.dve, lambda h: h.tensor_scalar(out=Hb[hi_][:], in0=Hb[hi_][:], scalar1=self.flag[0:64, 0:1], scalar2=None, op0=ALU.mult),
                                     reads=[f"C_Hb{hi_}", "flag_sb"], writes=[f"C_Hb{hi_}"])
                            yo = RY[0:64, ch * 64:(ch + 1) * 64]
                            k.op(k.pe, lambda h: h.matmul(yo, lhsT=Hb[hi_][:, :], rhs=RpT[:, ch, :], start=True, stop=False), reads=[f"C_Hb{hi_}", "C_RpT"], writes=["C_RY"])
                            k.op(k.pe, lambda h: h.matmul(yo, lhsT=tmV[:, ch, :], rhs=SB_[:, ch, 64:128], start=False, stop=False), reads=["C_tmV", "C_SB"], writes=["C_RY"], pe_acc=True)
                            k.op(k.pe, lambda h: h.matmul(yo, lhsT=Zn[:, ch, 64:128], rhs=SA[:, ch, 64:128], start=False, stop=True), reads=["C_Zn", "C_SA"], writes=["C_RY"], pe_acc=True)
                            ph, phn = nsml()
                            k.op(k.pe, lambda h: h.matmul(ph[0:64, 0:64], lhsT=PcT[:, ch, :], rhs=Hs[hi_][:, :], start=True, stop=True), reads=["C_PcT", f"C_H{hi_}"], writes=[phn])
                            k.op(k.dve, lambda h: h.tensor_tensor(out=Hs[ho_][:, :], in0=ph[0:64, 0:64], in1=Qc[:, ch, :], op=ALU.add), reads=[phn, "C_Qc"], writes=[f"C_H{ho_}"])
                            k.op(k.act, lambda h: h.copy(out=Hb[ho_][:, :], in_=Hs[ho_][:, :]), reads=[f"C_H{ho_}"], writes=[f"C_Hb{ho_}"])
                        if CSTAGE < 16:
                            continue
                        if fwd:
                            k.op(k.act, lambda h: h.copy(out=ysb[:, :], in_=RY[0:64, 0:W]), reads=["C_RY"], writes=["C_ysb"])
                            k.dma(k.pool, self.YF[f0:f0 + 64, t0:t0 + W], ysb[:, :], reads=["C_ysb"], writes=["YF"])
                            continue
                        k.dma(k.sp, yf[:, :], self.YF[f0:f0 + 64, t0:t0 + W], reads=["YF"], writes=["C_yf"])
                        k.op(k.dve, lambda h: h.tensor_tensor(out=ysb[:, :], in0=RY[0:64, 0:W], in1=yf[:, :], op=ALU.add), reads=["C_RY", "C_yf"], writes=["C_ysb"])
                        if CSTAGE < 17:
                            continue
                        ps1, ps1n = nsml()
                        mm_cols(ps1, onesf, lambda c0, c1: ysb[:, c0:c1], ["C_ysb", "ones_f"], ps1n)
                        k.op(k.dve, lambda h: h.scalar_tensor_tensor(out=t1[:, :], in0=ps1[0:64, 0:W], scalar=-1.0 / 64, in1=ysb[:, :], op0=ALU.mult, op1=ALU.add),
                             reads=[ps1n, "C_ysb"], writes=["C_t1"])
                        k.op(k.act, lambda h: h.activation(out=t2[:, :], in_=t1[:, :], func=AF.Square), reads=["C_t1"], writes=["C_t2"])
                        ps2, ps2n = nsml()
                        mm_cols(ps2, onesf, lambda c0, c1: t2[:, c0:c1], ["C_t2", "ones_f"], ps2n)
                        k.op(k.act, lambda h: h.activation(out=t2[:, :], in_=ps2[0:64, 0:W], func=AF.Sqrt, scale=1.0 / 64, bias=self.epsc[0:64, 1:2]),
                             reads=[ps2n, "epsc"], writes=["C_t2"])
                        k.op(k.dve, lambda h: h.reciprocal(out=t2[:, :], in_=t2[:, :]), reads=["C_t2"], writes=["C_t2"])
                        k.op(k.dve, lambda h: h.tensor_tensor(out=t1[:, :], in0=t1[:, :], in1=t2[:, :], op=ALU.mult), reads=["C_t1", "C_t2"], writes=["C_t1"])
                        k.op(k.dve, lambda h: h.tensor_scalar(out=t1[:, :], in0=t1[:, :], scalar1=pc("lnx_w", hd), scalar2=pc("lnx_b", hd), op0=ALU.mult, op1=ALU.add),
                             reads=["C_t1", "C_PRM"], writes=["C_t1"])
                        lora_sig(a2_, a2b, TA[0], AL, 0, hd, t0, pc("a0_0", hd), "C_a2")
                        k.op(k.pool, lambda h: h.tensor_tensor(out=a2_[:, :], in0=a2_[:, :], in1=a_[:, :], op=ALU.add), reads=["C_a2", "C_a"], writes=["C_a2"])
                        k.op(k.pool, lambda h: h.tensor_scalar(out=a2_[:, :], in0=a2_[:, :], scalar1=pc("k_a", hd), scalar2=pc("omka2", hd), op0=ALU.mult, op1=ALU.add),
                             reads=["C_a2", "C_PRM"], writes=["C_a2"])
                        k.op(k.pool, lambda h: h.tensor_tensor(out=a2_[:, :], in0=a2_[:, :], in1=kx[:, :], op=ALU.mult), reads=["C_a2", "C_k"], writes=["C_a2"])
                        k.op(k.dve, lambda h: h.scalar_tensor_tensor(out=a2_[:, :], in0=a2_[:, :], scalar=pc("r_k", hd), in1=r_[:, :], op0=ALU.mult, op1=ALU.mult),
                             reads=["C_a2", "C_r", "C_PRM"], writes=["C_a2"])
                        ps3, ps3n = nsml()
                        mm_cols(ps3, onesf, lambda c0, c1: a2_[:, c0:c1], ["C_a2", "ones_f"], ps3n)
                        k.op(k.dve, lambda h: h.tensor_tensor(out=t2[:, :], in0=ps3[0:64, 0:W], in1=v_[:, :], op=ALU.mult), reads=[ps3n, "C_v"], writes=["C_t2"])
                        k.op(k.dve, lambda h: h.tensor_tensor(out=t1[:, :], in0=t1[:, :], in1=t2[:, :], op=ALU.add), reads=["C_t1", "C_t2"], writes=["C_t1"])
                        pg_, pgn = nsml()
                        for c0 in range(0, W, 512):
                            c1 = min(W, c0 + 512)
                            for i in range(GKC):
                                gk = min(128, GL - i * 128)
                                k.op(k.pe, lambda h: h.matmul(pg_[0:64, c0:c1], lhsT=g2b[0:gk, i, f0:f0 + 64], rhs=TG[i][0:gk, t0 + c0:t0 + c1],
                                                              start=(i == 0), stop=(i == GKC - 1)), reads=["C_T", "C_g2b"], writes=[pgn], pe_acc=(i > 0))
                        k.op(k.dve, lambda h: h.tensor_tensor(out=ob[:, :], in0=pg_[0:64, 0:W], in1=t1[:, :], op=ALU.mult), reads=[pgn, "C_t1"], writes=["C_ob"])
                        k.dma(k.pool, self.MIXT[c["ATTW"] + f0:c["ATTW"] + f0 + 64, t0:t0 + W], ob[:, :], reads=["C_ob"], writes=["MIXT"])
                if fwd:
                    k.barrier()

    def phase_D(self, l, xs, xd):
        c, k = self.c, self.k
        D, T = c["D"], c["T"]
        KM = c["MIXW"] // 128
        NT = min(512, T)
        wo = self.Wb["w_out"]
        xsn, xdn = "X_" + str(id(xs)), "X_" + str(id(xd))
        with ExitStack() as es:
            MT = self.sb(es, "D_MT", [128, KM, NT], BF16)
            wt = [self.sb(es, f"D_w{i}", [128, KM, 512], BF16) for i in range(2)]
            xsl = [self.sb(es, f"D_x{i}", [128, 512], F32) for i in range(3)]
            pm = [self.ps(es, f"D_ps{i}", [128, 512], F32) for i in range(4)]
            cnt = 0
            wc = 0
            for t0 in range(0, T, NT):
                k.dma(k.sp, MT[:], self.MIXT[:, t0:t0 + NT].rearrange("(kc p) t -> p kc t", p=128), reads=["MIXT"], writes=["D_MT"])
                for cb in range(D // 512):
                    wi = wc % 2
                    wc += 1
                    k.dma(k.sp, wt[wi][:], wo[l * c["MIXW"]:(l + 1) * c["MIXW"], cb * 512:(cb + 1) * 512].rearrange("(kc p) n -> p kc n", p=128),
                          reads=["W_w_out"], writes=[f"D_w{wi}"])
                    for s in range(NT // 128):
                        i = cnt % 4
                        xi = cnt % 3
                        cnt += 1
                        r0 = t0 + s * 128
                        k.dma(k.sp, xsl[xi][:], xs[r0:r0 + 128, cb * 512:(cb + 1) * 512], reads=[xsn], writes=[f"D_x{xi}"])
                        for kc in range(KM):
                            k.op(k.pe, lambda h: h.matmul(pm[i][:, :], lhsT=MT[:, kc, s * 128:(s + 1) * 128], rhs=wt[wi][:, kc, :],
                                                          start=(kc == 0), stop=(kc == KM - 1)),
                                 reads=["D_MT", f"D_w{wi}"], writes=[f"D_ps{i}"], pe_acc=(kc > 0))
                        k.op(k.dve, lambda h: h.tensor_tensor(out=xsl[xi][:], in0=pm[i][:, :], in1=xsl[xi][:], op=ALU.add),
                             reads=[f"D_ps{i}", f"D_x{xi}"], writes=[f"D_x{xi}"])
                        k.dma(k.pool, xd[r0:r0 + 128, cb * 512:(cb + 1) * 512], xsl[xi][:], reads=[f"D_x{xi}"], writes=[xdn])

    def qknorm_evac(self, pm, pmn, n, gcol, sq, sqn, pq, pqn, rs, rsn, out_ap, outn, gname):
        k = self.k
        k.op(k.act, lambda h: h.activation(out=sq[:, 0:n], in_=pm[:, 0:n], func=AF.Square), reads=[pmn], writes=[sqn])
        k.op(k.pe, lambda h: h.matmul(pq[:, 0:n], lhsT=self.ones_b[:], rhs=sq[:, 0:n], start=True, stop=True),
             reads=[sqn, "ones_b"], writes=[pqn])
        k.op(k.act, lambda h: h.activation(out=rs[:, 0:n], in_=pq[:, 0:n], func=AF.Sqrt, scale=1.0 / 128, bias=self.epsc[:, 0:1]),
             reads=[pqn, "epsc"], writes=[rsn])
        k.op(k.dve, lambda h: h.reciprocal(out=rs[:, 0:n], in_=rs[:, 0:n]), reads=[rsn], writes=[rsn])
        k.op(k.dve, lambda h: h.scalar_tensor_tensor(out=out_ap, in0=pm[:, 0:n], scalar=gcol, in1=rs[:, 0:n], op0=ALU.mult, op1=ALU.mult),
             reads=[pmn, rsn, gname], writes=[outn])

    def phase_E(self, l, xs, xd):
        c, k = self.c, self.k
        D, T, S = c["D"], c["T"], c["S"]
        KC = D // 128
        NM = c["NMEM"]
        MH = c["MEMH"]
        MW = c["MEMW"]
        NT = min(256, T)
        scale = 128 ** -0.5
        xsn, xdn = "X_" + str(id(xs)), "X_" + str(id(xd))
        with ExitStack() as es:
            kmT = self.sb(es, "E_kmT", [128, 2, MH, NM], BF16)
            vm = self.sb(es, "E_vm", [128, 2, NM // 128, MW], BF16)
            qg = self.sb(es, "E_qg", [128, 2], F32)
            k.dma(k.sp, qg[:, 0:1], self.P["qn_mem"][l:l + 1, :].rearrange("o p -> p o"), reads=[], writes=["E_qg"])
            k.dma(k.sp, qg[:, 1:2], self.P["kn_mem"][l:l + 1, :].rearrange("o p -> p o"), reads=[], writes=["E_qg"])
            gain = self.sb(es, "E_gain", [128, D], F32)
            self.alloc_norm_state(es)
            sq = self.sb(es, "E_sq", [128, NM], BF16)
            rs = self.sb(es, "E_rs", [128, NM], F32)
            pm = [self.ps(es, f"E_ps{i}", [128, 512], F32) for i in range(2)]
            pq = self.ps(es, "E_pq", [128, 512], F32)
            with ExitStack() as es1:
                mT = self.sb(es1, "E_hT", [128, KC, NM], BF16)
                wk = self.sb(es1, "E_wk", [128, KC, MW], BF16)
                wv = self.sb(es1, "E_wv", [128, KC, MW], BF16)
                k.dma(k.sp, gain[:], self.P["norm_memkv"][l:l + 1, :].partition_broadcast(128), reads=[], writes=["E_gain"])
                k.dma(k.sp, wk[:], self.Wb["wk_mem"][l * D:(l + 1) * D, :].rearrange("(kc p) n -> p kc n", p=128), reads=["W_wk_mem"], writes=["E_wk"])
                k.dma(k.sp, wv[:], self.Wb["wv_mem"][l * D:(l + 1) * D, :].rearrange("(kc p) n -> p kc n", p=128), reads=["W_wv_mem"], writes=["E_wv"])
                cnt = 0
                for sl in range(2):
                    self.norm_transpose(es1, self.mem_in, sl * NM, NM, gain, mT, "E")
                    for hd in range(MH):
                        i = cnt % 2
                        cnt += 1
                        for kc in range(KC):
                            k.op(k.pe, lambda h: h.matmul(pm[i][:, 0:NM], lhsT=wk[:, kc, hd * 128:(hd + 1) * 128], rhs=mT[:, kc, :],
                                                          start=(kc == 0), stop=(kc == KC - 1)),
                                 reads=["E_hT", "E_wk"], writes=[f"E_ps{i}"], pe_acc=(kc > 0))
                        self.qknorm_evac(pm[i], f"E_ps{i}", NM, qg[:, 1:2], sq, "E_sq", pq, "E_pq", rs, "E_rs",
                                         kmT[:, sl, hd, :], "E_kmT", "E_qg")
                    for mt in range(NM // 128):
                        i = cnt % 2
                        cnt += 1
                        for kc in range(KC):
                            k.op(k.pe, lambda h: h.matmul(pm[i][:, 0:MW], lhsT=mT[:, kc, mt * 128:(mt + 1) * 128], rhs=wv[:, kc, :],
                                                          start=(kc == 0), stop=(kc == KC - 1)),
                                 reads=["E_hT", "E_wv"], writes=[f"E_ps{i}"], pe_acc=(kc > 0))
                        k.op(k.act, lambda h: h.copy(out=vm[:, sl, mt, :], in_=pm[i][:, 0:MW]), reads=[f"E_ps{i}"], writes=["E_vm"])
                k.barrier()
            with ExitStack() as es2:
                hT = self.sb(es2, "E_hT2", [128, KC, NT], BF16)
                wq = self.sb(es2, "E_wq", [128, KC, MW], BF16)
                wo = self.sb(es2, "E_wo", [128, MH, D], BF16)
                qT = self.sb(es2, "E_qT", [128, NT], BF16)
                pT = self.sb(es2, "E_pT", [128, NM // 128, NT], BF16)
                oT = self.sb(es2, "E_oT", [128, MH, NT], BF16)
                rd = self.sb(es2, "E_rd", [128, NT], F32)
                xsl = [self.sb(es2, f"E_x{i}", [128, 512], F32) for i in range(3)]
                pss = [self.ps(es2, f"E_pss{i}", [128, 512], F32) for i in range(2)]
                po = self.ps(es2, "E_po", [128, 512], F32)
                k.dma(k.sp, gain[:], self.P["norm_mem"][l:l + 1, :].partition_broadcast(128), reads=[], writes=["E_gain"])
                k.dma(k.sp, wq[:], self.Wb["wq_mem"][l * D:(l + 1) * D, :].rearrange("(kc p) n -> p kc n", p=128), reads=["W_wq_mem"], writes=["E_wq"])
                k.dma(k.sp, wo[:], self.Wb["wo_mem"][l * MW:(l + 1) * MW, :].rearrange("(kc p) n -> p kc n", p=128), reads=["W_wo_mem"], writes=["E_wo"])
                cnt = 0
                xc = 0
                for t0 in range(0, T, NT):
                    sl = t0 // S
                    self.norm_transpose(es2, xs, t0, NT, gain, hT, "E", xobj=xsn)
                    for hd in range(MH):
                        i = cnt % 2
                        cnt += 1
                        for kc in range(KC):
                            k.op(k.pe, lambda h: h.matmul(pm[i][:, 0:NT], lhsT=wq[:, kc, hd * 128:(hd + 1) * 128], rhs=hT[:, kc, :],
                                                          start=(kc == 0), stop=(kc == KC - 1)),
                                 reads=["E_hT", "E_wq"], writes=[f"E_ps{i}"], pe_acc=(kc > 0))
                        self.qknorm_evac(pm[i], f"E_ps{i}", NT, qg[:, 0:1], sq, "E_sq", pq, "E_pq", rs, "E_rs", qT[:, :], "E_qT", "E_qg")
                        for mt in range(NM // 128):
                            k.op(k.pe, lambda h: h.matmul(pss[mt][:, 0:NT], lhsT=kmT[:, sl, hd, mt * 128:(mt + 1) * 128], rhs=qT[:, :],
                                                          start=True, stop=True), reads=["E_kmT", "E_qT"], writes=[f"E_pss{mt}"])
                            k.op(k.act, lambda h: h.activation(out=pT[:, mt, :], in_=pss[mt][:, 0:NT], func=AF.Exp, scale=scale),
                                 reads=[f"E_pss{mt}"], writes=[f"E_pT{mt}"])
                        for mt in range(NM // 128):
                            k.op(k.pe, lambda h: h.matmul(po[:, 0:NT], lhsT=vm[:, sl, mt, hd * 128:(hd + 1) * 128], rhs=pT[:, mt, :],
                                                          start=(mt == 0), stop=(mt == NM // 128 - 1)),
                                 reads=["E_vm", f"E_pT{mt}"], writes=["E_po"], pe_acc=(mt > 0))
                        for mt in range(NM // 128):
                            k.op(k.pe, lambda h: h.matmul(po[:, 256:256 + NT], lhsT=self.ones_b[:], rhs=pT[:, mt, :],
                                                          start=(mt == 0), stop=(mt == NM // 128 - 1)),
                                 reads=["ones_b", f"E_pT{mt}"], writes=["E_pd"], pe_acc=(mt > 0))
                        k.op(k.dve, lambda h: h.reciprocal(out=rd[:], in_=po[:, 256:256 + NT]), reads=["E_pd"], writes=["E_rd"])
                        k.op(k.dve, lambda h: h.tensor_tensor(out=oT[:, hd, :], in0=po[:, 0:NT], in1=rd[:], op=ALU.mult),
                             reads=["E_po", "E_rd"], writes=["E_oT"])
                    for s in range(NT // 128):
                        r0 = t0 + s * 128
                        for cb in range(D // 512):
                            i = cnt % 2
                            cnt += 1
                            xi = xc % 3
                            xc += 1
                            k.dma(k.sp, xsl[xi][:], xs[r0:r0 + 128, cb * 512:(cb + 1) * 512], reads=[xsn], writes=[f"E_x{xi}"])
                            for hd in range(MH):
                                k.op(k.pe, lambda h: h.matmul(pm[i][:, :], lhsT=oT[:, hd, s * 128:(s + 1) * 128], rhs=wo[:, hd, cb * 512:(cb + 1) * 512],
                                                              start=(hd == 0), stop=(hd == MH - 1)),
                                     reads=["E_oT", "E_wo"], writes=[f"E_ps{i}"], pe_acc=(hd > 0))
                            k.op(k.dve, lambda h: h.tensor_tensor(out=xsl[xi][:], in0=pm[i][:, :], in1=xsl[xi][:], op=ALU.add),
                                 reads=[f"E_ps{i}", f"E_x{xi}"], writes=[f"E_x{xi}"])
                            k.dma(k.pool, xd[r0:r0 + 128, cb * 512:(cb + 1) * 512], xsl[xi][:], reads=[f"E_x{xi}"], writes=[xdn])

    def phase_F(self, l, xs, xd):
        c, k = self.c, self.k
        D, T, DFF = c["D"], c["T"], c["DFF"]
        KC = D // 128
        KF = DFF // 128
        NT = min(256, T)
        KG = 8
        xsn, xdn = "X_" + str(id(xs)), "X_" + str(id(xd))
        wd_ = self.Wb["w_down"]
        with ExitStack() as es:
            self.alloc_norm_state(es)
            gain = self.sb(es, "F_gain", [128, D], F32)
            k.dma(k.sp, gain[:], self.P["norm_ffn"][l:l + 1, :].partition_broadcast(128), reads=[], writes=["F_gain"])
            hT = self.sb(es, "F_hT", [128, KC, NT], BF16)
            aT = self.sb(es, "F_aT", [128, KF, NT], BF16)
            wg = [self.sb(es, f"F_wg{i}", [128, KC, 128], BF16) for i in range(2)]
            wu = [self.sb(es, f"F_wu{i}", [128, KC, 128], BF16) for i in range(2)]
            wd = [self.sb(es, f"F_wd{i}", [128, KG, 512], BF16) for i in range(2)]
            sg = [self.sb(es, f"F_sg{i}", [128, NT], F32) for i in range(2)]
            xsl = [self.sb(es, f"F_x{i}", [128, 512], F32) for i in range(3)]
            pg = [self.ps(es, f"F_pg{i}", [128, 512], F32) for i in range(2)]
            pu = [self.ps(es, f"F_pu{i}", [128, 512], F32) for i in range(2)]
            pd = [self.ps(es, f"F_pd{i}", [128, 512], F32) for i in range(NT // 128)]
            cnt = 0
            dc = 0
            xc = 0
            for t0 in range(0, T, NT):
                self.norm_transpose(es, xs, t0, NT, gain, hT, "F", xobj=xsn)
                for f in range(KF):
                    i = cnt % 2
                    cnt += 1
                    k.dma(k.sp, wg[i][:], self.wtile("w_gate", l, f * 128, 128), reads=["W_w_gate"], writes=[f"F_wg{i}"])
                    k.dma(k.sp, wu[i][:], self.wtile("w_up", l, f * 128, 128), reads=["W_w_up"], writes=[f"F_wu{i}"])
                    for kc in range(KC):
                        k.op(k.pe, lambda h: h.matmul(pg[i][:, 0:NT], lhsT=wg[i][:, kc, :], rhs=hT[:, kc, :], start=(kc == 0), stop=(kc == KC - 1)),
                             reads=["F_hT", f"F_wg{i}"], writes=[f"F_pg{i}"], pe_acc=(kc > 0))
                    for kc in range(KC):
                        k.op(k.pe, lambda h: h.matmul(pu[i][:, 0:NT], lhsT=wu[i][:, kc, :], rhs=hT[:, kc, :], start=(kc == 0), stop=(kc == KC - 1)),
                             reads=["F_hT", f"F_wu{i}"], writes=[f"F_pu{i}"], pe_acc=(kc > 0))
                    k.op(k.act, lambda h: h.activation(out=sg[i][:], in_=pg[i][:, 0:NT], func=AF.Silu), reads=[f"F_pg{i}"], writes=[f"F_sg{i}"])
                    k.op(k.dve, lambda h: h.tensor_tensor(out=aT[:, f, :], in0=pu[i][:, 0:NT], in1=sg[i][:], op=ALU.mult),
                         reads=[f"F_pu{i}", f"F_sg{i}"], writes=["F_aT"])
                for cb in range(D // 512):
                    ngr = (KF + KG - 1) // KG
                    for gi in range(ngr):
                        f0 = gi * KG
                        f1 = min(KF, f0 + KG)
                        di = dc % 2
                        dc += 1
                        k.dma(k.sp, wd[di][:, 0:f1 - f0, :],
                              wd_[l * DFF + f0 * 128:l * DFF + f1 * 128, cb * 512:(cb + 1) * 512].rearrange("(kc p) n -> p kc n", p=128),
                              reads=["W_w_down"], writes=[f"F_wd{di}"])
                        for s in range(NT // 128):
                            for f in range(f0, f1):
                                k.op(k.pe, lambda h: h.matmul(pd[s][:, :], lhsT=aT[:, f, s * 128:(s + 1) * 128], rhs=wd[di][:, f - f0, :],
                                                              start=(f == 0), stop=(f == KF - 1)),
                                     reads=["F_aT", f"F_wd{di}"], writes=[f"F_pd{s}"], pe_acc=(f > 0))
                    for s in range(NT // 128):
                        r0 = t0 + s * 128
                        xi = xc % 3
                        xc += 1
                        k.dma(k.sp, xsl[xi][:], xs[r0:r0 + 128, cb * 512:(cb + 1) * 512], reads=[xsn], writes=[f"F_x{xi}"])
                        k.op(k.dve, lambda h: h.tensor_tensor(out=xsl[xi][:], in0=pd[s][:, :], in1=xsl[xi][:], op=ALU.add),
                             reads=[f"F_pd{s}", f"F_x{xi}"], writes=[f"F_x{xi}"])
                        k.dma(k.pool, xd[r0:r0 + 128, cb * 512:(cb + 1) * 512], xsl[xi][:], reads=[f"F_x{xi}"], writes=[xdn])


_PROG_CACHE = {}


def host_inputs(inp, c, core_plan):
    Ld = c["DEPTH"]
    shared = {}
    for nm in ["w_in", "w_out", "wq_mem", "wk_mem", "wv_mem", "wo_mem", "w_gate", "w_up", "w_down", "w2", "a2", "g2"]:
        a = np.asarray(inp[nm], np.float32)
        shared[nm] = np.ascontiguousarray(a.reshape(-1, a.shape[-1]))
    for nm in ["norm_mix", "norm_mem", "norm_memkv", "norm_ffn", "q_norm", "k_norm", "qn_mem", "kn_mem", "sink",
               "shift_prev", "shift_next", "k_k", "k_a", "lnx_w", "lnx_b"]:
        shared[nm] = np.ascontiguousarray(np.asarray(inp[nm], np.float32))
    shared["w0"] = np.ascontiguousarray(np.asarray(inp["w0"], np.float32).reshape(Ld * 2, c["RW"]))
    shared["a0"] = np.ascontiguousarray(np.asarray(inp["a0"], np.float32).reshape(Ld * 2, c["RW"]))
    shared["r_k"] = np.ascontiguousarray(np.asarray(inp["r_k"], np.float32).reshape(Ld, c["RW"]))
    rb = np.asarray(inp["rel_bias"], np.float32)
    rb_ext = np.concatenate([rb, np.full((1, rb.shape[1]), NEG, np.float32)], axis=0)
    idx = bias_index_table()
    bt = rb_ext[idx]
    shared["biasT"] = np.ascontiguousarray(bt.transpose(1, 3, 0, 2).reshape(128, -1))
    shared["consts"] = make_consts_np()
    maps = []
    T, D, NM = c["T"], c["D"], c["NMEM"]
    for plan in core_plan:
        m = dict(shared)
        if plan is None:
            m["x"] = np.zeros((T, D), np.float32)
            m["mem"] = np.zeros((2 * NM, D), np.float32)
            m["flag"] = np.zeros((128, 1), np.float32)
        elif plan[0] == "p":
            b = plan[1]
            m["x"] = np.ascontiguousarray(inp["x_prompt"][b])
            m["mem"] = np.ascontiguousarray(np.concatenate([inp["mem_prompt"][b], inp["mem_prompt"][b]], axis=0))
            m["flag"] = np.ones((128, 1), np.float32)
        else:
            i0, i1 = plan[1], plan[2]
            m["x"] = np.ascontiguousarray(np.concatenate([inp["x_sample"][i0], inp["x_sample"][i1]], axis=0))
            m["mem"] = np.ascontiguousarray(np.concatenate([inp["mem_sample"][i0], inp["mem_sample"][i1]], axis=0))
            m["flag"] = np.zeros((128, 1), np.float32)
        maps.append(m)
    return maps


def run(inp, c, n_cores, nprompt, nsample, debug=False):
    S, T, D = c["S"], c["T"], c["D"]
    assert inp["x_prompt"].shape[1] == T and inp["x_sample"].shape[1] == S
    plan = [("p", b) for b in range(nprompt)]
    for i in range(0, nsample, 2):
        plan.append(("s", i, i + 1))
    while len(plan) < n_cores:
        plan.append(None)
    assert len(plan) == n_cores
    key = (tuple(sorted(c.items())), debug)
    if key not in _PROG_CACHE:
        p1 = Prog(c, debug=debug)
        ms = set(p1.k.rec)
        del p1
        _PROG_CACHE[key] = Prog(c, debug=debug, milestones=ms)
    prog = _PROG_CACHE[key]
    maps = host_inputs(inp, c, plan)
    res = run_bass_kernel_spmd(prog.nc, maps, core_ids=list(range(n_cores)))
    yp = np.zeros((nprompt, T, D), np.float32)
    ys = np.zeros((nsample, S, D), np.float32)
    for ci, pl in enumerate(plan):
        if pl is None:
            continue
        y = np.asarray(res.results[ci]["y"])
        if pl[0] == "p":
            yp[pl[1]] = y
        else:
            ys[pl[1]] = y[0:S]
            ys[pl[2]] = y[S:2 * S]
    dbg = [{nm: np.asarray(r[nm]) for nm in prog.dbg_outs} for r in res.results] if debug else None
    if debug:
        return (yp, ys), dbg
    return (yp, ys)


def kernel(**inputs):
    c = derived(FULL_CFG)
    inp = {k_: np.asarray(v) for k_, v in inputs.items()}
    return run(inp, c, n_cores=8, nprompt=2, nsample=8)
```
